# Optimizing a Trainium2 kernel written in Bass

```python
import math
import jax, jax.numpy as jnp
from jax import lax
import numpy as np

D_MODEL = 2048
BATCH = 1
SEQ = 16384
DEPTH = 1

GLA_HEADS = 4
GLA_DK = 128
GLA_DV = 256
GLA_RANK = 16
GLA_TAU = 16.0
GLA_CHUNK = 64
DIFF_HEADS = 4
DIFF_DQK = 128
DIFF_DV = 2 * DIFF_DQK
Q_BLOCK = 128
D_FF = 5632
CONV_W = 3
EPS = 1e-6

GLA_QK = GLA_HEADS * GLA_DK
GLA_V = GLA_HEADS * GLA_DV
DIFF_QK = DIFF_HEADS * 2 * DIFF_DQK
DIFF_V = DIFF_HEADS * DIFF_DV
MIX_WIDTH = GLA_V + DIFF_V
IN_SPLITS = (GLA_QK, GLA_QK, GLA_V, GLA_V, GLA_RANK, DIFF_QK, DIFF_QK, DIFF_V)
IN_COLS = sum(IN_SPLITS)

kernel_name = 'hybrid_gla_diffattn_convffn'


def rmsnorm(x, g):
    xf = x.astype(jnp.float32)
    y = xf * lax.rsqrt(jnp.mean(xf * xf, axis=-1, keepdims=True) + EPS)
    return (y * g.astype(jnp.float32)).astype(x.dtype)


def gla_chunked(q, k, v, log_a):
    B, S = q.shape[0], q.shape[1]
    n = S // GLA_CHUNK

    def to_chunks(t):
        t = t.astype(jnp.float32).reshape(B, n, GLA_CHUNK, GLA_HEADS, t.shape[-1])
        return jnp.transpose(t, (1, 0, 3, 2, 4))

    qc = to_chunks(q * (GLA_DK ** -0.5))
    kc, vc, ac = to_chunks(k), to_chunks(v), to_chunks(log_a)
    causal = jnp.tril(jnp.ones((GLA_CHUNK, GLA_CHUNK), dtype=bool))

    def step(state, inp):
        qi, ki, vi, ai = inp
        b = jnp.cumsum(ai, axis=2)
        o_inter = jnp.einsum('bhcd,bhde->bhce', qi * jnp.exp(b), state)
        diff = b[:, :, :, None, :] - b[:, :, None, :, :]
        decay = jnp.exp(jnp.where(causal[:, :, None], diff, -jnp.inf))
        scores = jnp.einsum('bhid,bhjd,bhijd->bhij', qi, ki, decay)
        o_intra = jnp.einsum('bhij,bhje->bhie', scores, vi)
        b_last = b[:, :, -1:, :]
        state = (jnp.exp(b_last[:, :, 0, :])[..., None] * state
                 + jnp.einsum('bhcd,bhce->bhde', ki * jnp.exp(b_last - b), vi))
        return state, o_inter + o_intra

    s0 = jnp.zeros((B, GLA_HEADS, GLA_DK, GLA_DV), jnp.float32)
    _, o = lax.scan(step, s0, (qc, kc, vc, ac))
    return jnp.transpose(o, (1, 0, 3, 2, 4)).reshape(B, S, GLA_HEADS, GLA_DV)


def diff_attention(q, k, v, lam):
    B, S = q.shape[0], q.shape[1]
    nb = S // Q_BLOCK
    slopes = jnp.asarray(2.0 ** (-8.0 * np.arange(1, DIFF_HEADS + 1) / DIFF_HEADS), jnp.float32)
    qb = jnp.transpose(q.reshape(B, nb, Q_BLOCK, DIFF_HEADS, 2, DIFF_DQK), (1, 0, 2, 3, 4, 5))
    k_pos = jnp.arange(S, dtype=jnp.int32)
    scale = DIFF_DQK ** -0.5

    def block(args):
        qi, bi = args
        q_pos = bi * Q_BLOCK + jnp.arange(Q_BLOCK, dtype=jnp.int32)
        dist = q_pos[:, None] - k_pos[None, :]
        bias = -slopes[:, None, None] * dist.astype(jnp.float32)
        s = jnp.einsum('bqhcd,bkhcd->bhcqk', qi, k,
                       preferred_element_type=jnp.float32) * scale + bias[None, :, None]
        s = jnp.where(dist >= 0, s, -jnp.inf)
        p = jax.nn.softmax(s, axis=-1)
        a = p[:, :, 0] - lam * p[:, :, 1]
        return jnp.einsum('bhqk,bkhe->bqhe', a, v.astype(jnp.float32))

    o = lax.map(block, (qb, jnp.arange(nb, dtype=jnp.int32)))
    return jnp.transpose(o, (1, 0, 2, 3, 4)).reshape(B, S, DIFF_HEADS, DIFF_DV)


def token_mixer(h, w_in, w_alpha_up, b_alpha, gla_norm, lambda_q1, lambda_k1,
                lambda_q2, lambda_k2, diff_norm, w_o, lambda_init):
    B, S, _ = h.shape
    proj = h @ w_in
    offs = np.cumsum((0,) + IN_SPLITS)
    gq, gk, gv, gg, ga, dq, dk, dv = [proj[..., int(offs[i]):int(offs[i + 1])]
                                      for i in range(len(IN_SPLITS))]
    log_a = jax.nn.log_sigmoid((ga @ w_alpha_up + b_alpha).astype(jnp.float32)) / GLA_TAU
    o_a = gla_chunked(gq.reshape(B, S, GLA_HEADS, GLA_DK), gk.reshape(B, S, GLA_HEADS, GLA_DK),
                      gv.reshape(B, S, GLA_HEADS, GLA_DV), log_a.reshape(B, S, GLA_HEADS, GLA_DK))
    gate = jax.nn.silu(gg.astype(jnp.float32)).reshape(B, S, GLA_HEADS, GLA_DV)
    o_a = (rmsnorm(o_a, gla_norm) * gate).reshape(B, S, GLA_V)
    f32 = jnp.float32
    lam = (jnp.exp(jnp.sum(lambda_q1.astype(f32) * lambda_k1.astype(f32)))
           - jnp.exp(jnp.sum(lambda_q2.astype(f32) * lambda_k2.astype(f32))) + lambda_init)
    o_b = diff_attention(dq.reshape(B, S, DIFF_HEADS, 2, DIFF_DQK),
                         dk.reshape(B, S, DIFF_HEADS, 2, DIFF_DQK),
                         dv.reshape(B, S, DIFF_HEADS, DIFF_DV), lam)
    o_b = (rmsnorm(o_b, diff_norm) * (1.0 - lambda_init)).reshape(B, S, DIFF_V)
    o = jnp.concatenate([o_a, o_b], axis=-1).astype(h.dtype)
    return o @ w_o


def conv_ffn(h, w_ffn_in, conv_w, conv_b, w_ffn_out):
    S = h.shape[1]
    up = h @ w_ffn_in
    a, b = up[..., :D_FF], up[..., D_FF:]
    a_pad = jnp.pad(a, ((0, 0), (CONV_W - 1, 0), (0, 0)))
    a = (conv_w[0] * a_pad[:, 0:S] + conv_w[1] * a_pad[:, 1:S + 1]
         + conv_w[2] * a_pad[:, 2:S + 2] + conv_b)
    return (jax.nn.gelu(a, approximate=True) * b) @ w_ffn_out


def setup_inputs(seed: int = 0) -> dict:
    key = jax.random.key(seed)
    ks = jax.random.split(key, 20)
    nrm = lambda k, shape: jax.random.normal(k, shape, jnp.float32)
    gain = lambda k, n: 1.0 + 0.1 * nrm(k, (DEPTH, n))
    return {
        'x': nrm(ks[0], (BATCH, SEQ, D_MODEL)),
        'attn_pre_norm': gain(ks[1], D_MODEL),
        'w_in': nrm(ks[2], (DEPTH, D_MODEL, IN_COLS)) * D_MODEL ** -0.5,
        'w_alpha_up': nrm(ks[3], (DEPTH, GLA_RANK, GLA_QK)) * GLA_RANK ** -0.5,
        'b_alpha': 0.1 * nrm(ks[4], (DEPTH, GLA_QK)),
        'gla_norm': gain(ks[5], GLA_DV),
        'lambda_q1': 0.1 * nrm(ks[6], (DEPTH, DIFF_DQK)),
        'lambda_k1': 0.1 * nrm(ks[7], (DEPTH, DIFF_DQK)),
        'lambda_q2': 0.1 * nrm(ks[8], (DEPTH, DIFF_DQK)),
        'lambda_k2': 0.1 * nrm(ks[9], (DEPTH, DIFF_DQK)),
        'diff_norm': gain(ks[10], DIFF_DV),
        'w_o': nrm(ks[11], (DEPTH, MIX_WIDTH, D_MODEL)) * MIX_WIDTH ** -0.5,
        'attn_post_norm': gain(ks[12], D_MODEL),
        'ffn_pre_norm': gain(ks[13], D_MODEL),
        'w_ffn_in': nrm(ks[14], (DEPTH, D_MODEL, 2 * D_FF)) * D_MODEL ** -0.5,
        'conv_w': nrm(ks[15], (DEPTH, CONV_W, D_FF)) * CONV_W ** -0.5,
        'conv_b': 0.02 * nrm(ks[16], (DEPTH, D_FF)),
        'w_ffn_out': nrm(ks[17], (DEPTH, D_FF, D_MODEL)) * D_FF ** -0.5,
        'ffn_post_norm': gain(ks[18], D_MODEL),
    }


def reference(x, attn_pre_norm, w_in, w_alpha_up, b_alpha, gla_norm, lambda_q1, lambda_k1,
              lambda_q2, lambda_k2, diff_norm, w_o, attn_post_norm, ffn_pre_norm, w_ffn_in,
              conv_w, conv_b, w_ffn_out, ffn_post_norm):
    for l in range(DEPTH):
        lambda_init = 0.8 - 0.6 * math.exp(-0.3 * l)
        h = rmsnorm(x, attn_pre_norm[l])
        m = token_mixer(h, w_in[l], w_alpha_up[l], b_alpha[l], gla_norm[l], lambda_q1[l],
                        lambda_k1[l], lambda_q2[l], lambda_k2[l], diff_norm[l], w_o[l], lambda_init)
        x = x + rmsnorm(m, attn_post_norm[l])
        h = rmsnorm(x, ffn_pre_norm[l])
        f = conv_ffn(h, w_ffn_in[l], conv_w[l], conv_b[l], w_ffn_out[l])
        x = x + rmsnorm(f, ffn_post_norm[l])
    return x
```

```python
import numpy as np
import ml_dtypes
from contextlib import ExitStack
import concourse.bass as bass
import concourse.mybir as mybir
from concourse.bass_utils import run_bass_kernel_spmd

F32 = mybir.dt.float32
BF16 = mybir.dt.bfloat16
AF = mybir.ActivationFunctionType
ALU = mybir.AluOpType
bf16_np = ml_dtypes.bfloat16

NCORES = 8
D = 2048
KC = 16
DFF = 5632
NFF = 44
EPS = 1e-6
C_GQ, C_GK, C_GV, C_GG, C_GA, C_DQ, C_DK, C_DV = 0, 512, 1024, 2048, 3072, 3088, 4112, 5136
SLOPES = [2.0 ** (-8.0 * (h + 1) / 4) for h in range(4)]
LAMBDA_INIT = 0.8 - 0.6 * 1.0
NEG = -1.0e6

ENGS = ("pe", "act", "dve", "pool", "sp")


class Buf:
    __slots__ = ("name", "wt", "rts", "sem", "semcnt")

    def __init__(self, name):
        self.name = name
        self.wt = {}
        self.rts = {}
        self.sem = None
        self.semcnt = 0


class Sched:
    def __init__(self, nc, stack):
        self.nc = nc
        self.stack = stack
        self.streams = {e: [] for e in ENGS}
        self.esem = {e: stack.enter_context(nc.semaphore("s_" + e)) for e in ENGS}
        self.cnt = {e: 0 for e in ENGS}
        self.seen = {e: {} for e in ENGS}
        self.semh = dict(self.esem)
        self.nsem = 0
        self.allbufs = []
        self.ninst = 0

    def buf(self, name):
        b = Buf(name)
        self.allbufs.append(b)
        return b

    def _bufsem(self, b):
        if b.sem is None:
            self.nsem += 1
            key = "d%d" % self.nsem
            h = self.stack.enter_context(self.nc.semaphore(key))
            b.sem = key
            self.semh[key] = h
        return b.sem

    def _need(self, eng, tickets):
        best = {}
        for t in tickets:
            if t is None:
                continue
            k, v = t
            if v > best.get(k, 0):
                best[k] = v
        for k, v in best.items():
            if self.seen[eng].get(k, 0) >= v:
                continue
            self.seen[eng][k] = v
            h = self.semh[k]
            self.ninst += 1
            self.streams[eng].append(lambda e, h=h, v=v: e.wait_ge(h, v))

    def _deps(self, eng, reads, writes):
        ts = []
        for b in reads:
            ts.extend(b.wt.items())
        for b in writes:
            ts.extend(b.wt.items())
            ts.extend(b.rts.items())
        out = []
        for t in ts:
            if t is None:
                continue
            if t[0] == eng and eng == "pe":
                continue
            out.append(t)
        return out

    def op(self, eng, fn, reads=(), writes=(), inc=True):
        self._need(eng, self._deps(eng, reads, writes))
        self.ninst += 1
        if inc:
            self.cnt[eng] += 1
            v = self.cnt[eng]
            h = self.esem[eng]
            self.streams[eng].append(lambda e, fn=fn, h=h: fn(e).then_inc(h, 1))
        else:
            v = self.cnt[eng] + 1
            self.streams[eng].append(lambda e, fn=fn: fn(e))
        t = (eng, v)
        for b in reads:
            if b.rts.get(t[0], 0) < t[1]:
                b.rts[t[0]] = t[1]
        for b in writes:
            if b.wt.get(t[0], 0) < t[1]:
                b.wt[t[0]] = t[1]
            b.rts = {}
        return t

    def dma(self, q, out, in_, track, reads=(), writes=()):
        self._need(q, self._deps(q, reads, writes))
        key = self._bufsem(track)
        track.semcnt += 16
        v = track.semcnt
        h = self.semh[key]
        self.ninst += 1
        self.streams[q].append(lambda e, out=out, in_=in_, h=h: e.dma_start(out=out, in_=in_).then_inc(h, 16))
        t = (key, v)
        for b in reads:
            if b.rts.get(t[0], 0) < t[1]:
                b.rts[t[0]] = t[1]
        for b in writes:
            if b.wt.get(t[0], 0) < t[1]:
                b.wt[t[0]] = t[1]
            b.rts = {}
        return t

    def wait(self, eng, tickets):
        self._need(eng, [t for t in tickets if t is not None])

    def barrier(self):
        ts = [(e, self.cnt[e]) for e in ENGS if self.cnt[e] > 0]
        for b in self.allbufs:
            if b.sem is not None and b.semcnt > 0:
                ts.append((b.sem, b.semcnt))
        for e in ENGS:
            self._need(e, [t for t in ts if t[0] != e])

    def emit(self):
        nc = self.nc
        streams = self.streams
        self.streams = {e: [] for e in ENGS}
        with nc.Block() as block:
            @block.tensor
            def _(e):
                for f in streams["pe"]:
                    f(e)

            @block.scalar
            def _(e):
                for f in streams["act"]:
                    f(e)

            @block.vector
            def _(e):
                for f in streams["dve"]:
                    f(e)

            @block.gpsimd
            def _(e):
                for f in streams["pool"]:
                    f(e)

            @block.sync
            def _(e):
                for f in streams["sp"]:
                    f(e)


def build(nb=16, stages="ABCDE", debug=False):
    NT = NCORES * nb
    S_ALL = NT * 128
    NOWN = nb + 1
    S_OWN = NOWN * 128
    NG = 1 + nb // 2
    import os
    ELEVEL = float(os.environ.get("ELEVEL", "9"))
    nc = bass.Bass("TRN2", target_bir_lowering=False)

    def din(name, shape, dt=F32):
        return nc.dram_tensor(name, list(shape), dt, kind="ExternalInput").ap()

    x_all = din("x_all", [S_ALL, D])
    x_own = din("x_own", [S_OWN, D])
    w_in = din("w_in", [D, 6160])
    walpha = din("walpha", [17, 512])
    w_o = din("w_o", [D, D])
    w_f1 = din("w_f1", [D, 2 * DFF])
    w_f2 = din("w_f2", [DFF, D])
    nrm = din("nrm", [4, D])
    hnrm = din("hnrm", [2, 256])
    lamv = din("lamv", [4, 128])
    convp = din("convp", [128, 4, NFF])
    ident_d = din("ident", [128, 128], BF16)
    tri_d = din("tri", [4, 128, 128])
    sel_d = din("sel", [128, 8])
    flag_d = din("flag", [128, 1])
    posrel_d = din("posrel", [128, NG * NT])
    maskp_d = din("maskp", [128, 16, 256], BF16)
    maskh_d = din("maskh", [128, 8, 128], BF16)
    y_out = nc.dram_tensor("y", [nb * 128, D], F32, kind="ExternalOutput").ap()
    dbg = {}

    def dout(name, shape, dt=F32):
        ap = nc.dram_tensor(name, list(shape), dt, kind="ExternalOutput").ap()
        dbg[name] = ap
        return ap

    KT = nc.dram_tensor("KT", [8, 128, S_ALL], BF16, kind="Internal").ap()
    VV = nc.dram_tensor("VV", [S_ALL, 1024], BF16, kind="Internal").ap()
    QT = nc.dram_tensor("QT", [8, 128, S_OWN], BF16, kind="Internal").ap()
    OM = nc.dram_tensor("OM", [S_OWN, D], BF16, kind="Internal").ap()
    X1 = nc.dram_tensor("X1", [S_OWN, D], F32, kind="Internal").ap()
    SOWN = nc.dram_tensor("SOWN", [128, 1024], F32, kind="Internal").ap()

    with ExitStack() as top:
        S = Sched(nc, top)
        bKT, bVV, bQT, bOM, bX1, bSOWN = (S.buf(n) for n in ("KT", "VV", "QT", "OM", "X1", "SOWN"))
        pbank = [top.enter_context(nc.psum_tensor("pb%d" % i, [128, 512], F32)) for i in range(8)]

        def w_view(w_ap, c0, ncol):
            return w_ap[:, c0:c0 + ncol].rearrange("(c p) n -> p c n", p=128)

        def rstd_ops(t, bt, n):
            S.op("act", lambda e: e.activation(out=t[:, 1:2], in_=t[:, 0:1], func=AF.Ln, scale=1.0 / n, bias=epsc[:, 0:1]),
                 reads=[bt, bepsc], writes=[bt])
            S.op("act", lambda e: e.activation(out=t[:, 1:2], in_=t[:, 1:2], func=AF.Exp, scale=-0.5),
                 reads=[bt], writes=[bt])

        epsc = top.enter_context(nc.sbuf_tensor("epsc", [128, 2], F32)); bepsc = S.buf("epsc")
        S.op("pool", lambda e: e.memset(epsc[:, 0:1], EPS), writes=[bepsc])
        S.op("pool", lambda e: e.memset(epsc[:, 1:2], 1.0), writes=[bepsc])

        if "A" in stages:
            with ExitStack() as st:
                sb = lambda n, s, d: st.enter_context(nc.sbuf_tensor("A_" + n, list(s), d))
                wk = sb("wk", [128, KC, 1024], BF16); bwk = S.buf("wk")
                wv = sb("wv", [128, KC, 1024], BF16); bwv = S.buf("wv")
                wgk = sb("wgk", [128, KC, 512], BF16); bwgk = S.buf("wgk")
                wgv = sb("wgv", [128, KC, 1024], BF16); bwgv = S.buf("wgv")
                wga = sb("wga", [128, KC, 16], BF16); bwga = S.buf("wga")
                gbc = sb("gbc", [128, D], F32); bgbc = S.buf("gbc")
                wal = sb("wal", [17, 512], F32); bwal = S.buf("wal")
                ident = sb("ident", [128, 128], BF16); bid = S.buf("ident")
                tri = sb("tri", [128, 4, 128], F32); btri = S.buf("tri")
                negcol = sb("negcol", [128, 1], F32); bneg = S.buf("negcol")
                selt = sb("selt", [128, 8], F32); bsel = S.buf("selt")
                xs = [sb("xs%d" % i, [128, D], F32) for i in range(2)]; bxs = [S.buf("xs%d" % i) for i in range(2)]
                ssq = [sb("ssq%d" % i, [128, 2], F32) for i in range(2)]; bssq = [S.buf("ssq%d" % i) for i in range(2)]
                hb = [sb("hb%d" % i, [128, D], BF16) for i in range(2)]; bhb = [S.buf("hb%d" % i) for i in range(2)]
                hT = [sb("hT%d" % i, [128, KC, 512], BF16) for i in range(2)]; bhT = [S.buf("hT%d" % i) for i in range(2)]
                kst = [sb("kst%d" % i, [128, 512], BF16) for i in range(2)]; bkst = [S.buf("kst%d" % i) for i in range(2)]
                vst = [sb("vst%d" % i, [128, 1024], BF16) for i in range(2)]; bvst = [S.buf("vst%d" % i) for i in range(2)]
                gaT = sb("gaT", [17, 512], F32); bgaT = S.buf("gaT")
                e1 = sb("e1", [128, 512], F32); be1 = S.buf("e1")
                spt = sb("spt", [128, 512], F32); bspt = S.buf("spt")
                E3 = sb("E3", [128, 512], F32); bE3 = S.buf("E3")
                dec = sb("dec", [128, 4], F32); bdec = S.buf("dec")
                khat = sb("khat", [128, 512], BF16); bkhat = S.buf("khat")
                gvb = sb("gvb", [128, 1024], BF16); bgvb = S.buf("gvb")
                Sst = sb("Sst", [128, 1024], F32); bS = S.buf("Sst")
                Sown = sb("Sown", [128, 1024], F32); bSo = S.buf("Sown")
                bpb = [S.buf("pbA%d" % i) for i in range(8)]

                S.dma("sp", ident[:], ident_d, bid, writes=[bid])
                S.dma("sp", tri[:], tri_d.rearrange("a p n -> p a n"), btri, writes=[btri])
                S.dma("sp", selt[:], sel_d, bsel, writes=[bsel])
                S.dma("sp", wal[:], walpha, bwal, writes=[bwal])
                S.dma("sp", gbc[:], nrm[0:1, :].partition_broadcast(128), bgbc, writes=[bgbc])
                S.op("pool", lambda e: e.memset(negcol[:], -1.0 / 16.0), writes=[bneg])
                S.op("pool", lambda e: e.memset(gaT[:], 1.0), writes=[bgaT])
                S.op("pool", lambda e: e.memset(Sst[:], 0.0), writes=[bS])
                S.op("pool", lambda e: e.memset(Sown[:], 0.0), writes=[bSo])
                for kc in range(KC):
                    S.dma("pool", wk[:, kc, :], w_view(w_in, C_DK, 1024)[:, kc, :], bwk, writes=[bwk])
                for kc in range(KC):
                    S.dma("pool", wv[:, kc, :], w_view(w_in, C_DV, 1024)[:, kc, :], bwv, writes=[bwv])
                for kc in range(KC):
                    S.dma("pool", wgk[:, kc, :], w_view(w_in, C_GK, 512)[:, kc, :], bwgk, writes=[bwgk])
                for kc in range(KC):
                    S.dma("pool", wgv[:, kc, :], w_view(w_in, C_GV, 1024)[:, kc, :], bwgv, writes=[bwgv])
                S.dma("pool", wga[:], w_view(w_in, C_GA, 16), bwga, writes=[bwga])

                ptr_bf = [pbank[0][:].bitcast(BF16), pbank[1][:].bitcast(BF16)]
                cp_tgl = [0]

                def evac(out, in_, reads, writes, eng=None):
                    if eng is None:
                        eng = ("act", "dve")[cp_tgl[0] % 2]
                        cp_tgl[0] += 1
                    if eng == "act":
                        return S.op("act", lambda e: e.activation(out=out, in_=in_, func=AF.Copy), reads=reads, writes=writes)
                    return S.op("dve", lambda e: e.tensor_copy(out=out, in_=in_), reads=reads, writes=writes)

                snap_tiles = {nb * cc - 2: cc for cc in range(1, 8)}
                for M in range(NT // 4):
                    ms = M % 2
                    for j in range(4):
                        t = 4 * M + j
                        sl = t % 2
                        S.dma("sp", xs[sl][:], x_all[128 * t:128 * t + 128, :], bxs[sl], writes=[bxs[sl]])
                        S.op("pool", lambda e, sl=sl: e.memset(ssq[sl][:], 0.0), writes=[bssq[sl]])
                        S.op("act", lambda e, sl=sl: e.activation(out=hb[sl][:], in_=xs[sl][:], func=AF.Square,
                                                                 accum_out=ssq[sl][:, 0:1]),
                             reads=[bxs[sl]], writes=[bhb[sl], bssq[sl]])
                        rstd_ops(ssq[sl], bssq[sl], D)
                        S.op("dve", lambda e, sl=sl: e.scalar_tensor_tensor(out=hb[sl][:], in0=xs[sl][:], scalar=ssq[sl][:, 1:2],
                                                                           in1=gbc[:], op0=ALU.mult, op1=ALU.mult),
                             reads=[bxs[sl], bssq[sl], bgbc], writes=[bhb[sl]])
                        for half in range(2):
                            pt = ptr_bf[half]
                            for k8 in range(8):
                                kc = half * 8 + k8
                                S.op("pe", lambda e, pt=pt, k8=k8, kc=kc, sl=sl: e.transpose(
                                    out=pt[:, 128 * k8:128 * k8 + 128], in_=hb[sl][:, 128 * kc:128 * kc + 128], identity=ident[:]),
                                    reads=[bhb[sl], bid], writes=[bpb[half]], inc=(k8 == 7))
                            evac(hT[ms][:, half * 8:half * 8 + 8, 128 * j:128 * j + 128],
                                 pt.rearrange("p (k n) -> p k n", k=8), [bpb[half]], [bhT[ms]])
                    for cc in range(8):
                        pk = pbank[2 + cc % 2]; bpk = bpb[2 + cc % 2]
                        for kc in range(KC):
                            S.op("pe", lambda e, pk=pk, cc=cc, kc=kc, ms=ms: e.matmul(
                                pk[:, :], lhsT=wk[:, kc, 128 * cc:128 * cc + 128], rhs=hT[ms][:, kc, :],
                                start=(kc == 0), stop=(kc == KC - 1)),
                                reads=[bwk, bhT[ms]], writes=[bpk], inc=(kc == KC - 1))
                        ks = cc % 2
                        evac(kst[ks][:], pk[:, :], [bpk], [bkst[ks]])
                        S.dma("sp", KT[cc, :, 512 * M:512 * M + 512], kst[ks][:], bkst[ks], reads=[bkst[ks]], writes=[bKT])
                    for j in range(4):
                        t = 4 * M + j
                        vs = j % 2
                        for cb in range(2):
                            pv = pbank[4 + cb]; bpv = bpb[4 + cb]
                            for kc in range(KC):
                                S.op("pe", lambda e, pv=pv, cb=cb, kc=kc, ms=ms, j=j: e.matmul(
                                    pv[:, :], lhsT=hT[ms][:, kc, 128 * j:128 * j + 128], rhs=wv[:, kc, 512 * cb:512 * cb + 512],
                                    start=(kc == 0), stop=(kc == KC - 1)),
                                    reads=[bwv, bhT[ms]], writes=[bpv], inc=(kc == KC - 1))
                            evac(vst[vs][:, 512 * cb:512 * cb + 512], pv[:, :], [bpv], [bvst[vs]])
                        S.dma("sp", VV[128 * t:128 * t + 128, :], vst[vs][:], bvst[vs], reads=[bvst[vs]], writes=[bVV])
                    pg = pbank[6]; bpg = bpb[6]
                    for kc in range(KC):
                        S.op("pe", lambda e, kc=kc, ms=ms: e.matmul(
                            pg[0:16, :], lhsT=wga[:, kc, :], rhs=hT[ms][:, kc, :], start=(kc == 0), stop=(kc == KC - 1)),
                            reads=[bwga, bhT[ms]], writes=[bpg], inc=(kc == KC - 1))
                    evac(gaT[0:16, :], pg[0:16, :], [bpg], [bgaT], eng="dve")
                    for j in range(4):
                        t = 4 * M + j
                        pz = pbank[7]; bpz = bpb[7]
                        S.op("pe", lambda e, j=j: e.matmul(pz[:, :], lhsT=gaT[:, 128 * j:128 * j + 128], rhs=wal[:, :],
                                                           start=True, stop=True),
                             reads=[bgaT, bwal], writes=[bpz])
                        S.op("act", lambda e: e.activation(out=e1[:], in_=pz[:, :], func=AF.Exp, scale=-1.0),
                             reads=[bpz], writes=[be1])
                        S.op("act", lambda e: e.activation(out=spt[:], in_=e1[:], func=AF.Ln, bias=epsc[:, 1:2]),
                             reads=[be1, bepsc], writes=[bspt])
                        S.op("pe", lambda e: e.matmul(pz[:, :], lhsT=tri[:, 0, :], rhs=spt[:], start=True, stop=True),
                             reads=[btri, bspt], writes=[bpz])
                        S.op("act", lambda e: e.activation(out=E3[:], in_=pz[:, :], func=AF.Exp),
                             reads=[bpz], writes=[bE3])
                        for h in range(4):
                            S.op("pe", lambda e, h=h: e.matmul(pg[:, 4 * h:4 * h + 1], lhsT=spt[:, 128 * h:128 * h + 128],
                                                               rhs=negcol[:], start=True, stop=True),
                                 reads=[bspt, bneg], writes=[bpg], inc=(h == 3))
                        S.op("act", lambda e: e.activation(out=dec[:], in_=pg[:, 0:16].rearrange("p (h f) -> p h f", f=4)[:, :, 0],
                                                           func=AF.Exp),
                             reads=[bpg], writes=[bdec])
                        pk = pbank[2]; bpk = bpb[2]
                        for kc in range(KC):
                            S.op("pe", lambda e, kc=kc, ms=ms, j=j: e.matmul(
                                pk[:, :], lhsT=hT[ms][:, kc, 128 * j:128 * j + 128], rhs=wgk[:, kc, :],
                                start=(kc == 0), stop=(kc == KC - 1)),
                                reads=[bwgk, bhT[ms]], writes=[bpk], inc=(kc == KC - 1))
                        S.op("dve", lambda e: e.tensor_tensor(out=khat[:], in0=pk[:, :], in1=E3[:], op=ALU.mult),
                             reads=[bpk, bE3], writes=[bkhat])
                        for cb in range(2):
                            pv = pbank[4 + cb]; bpv = bpb[4 + cb]
                            for kc in range(KC):
                                S.op("pe", lambda e, pv=pv, cb=cb, kc=kc, ms=ms, j=j: e.matmul(
                                    pv[:, :], lhsT=hT[ms][:, kc, 128 * j:128 * j + 128], rhs=wgv[:, kc, 512 * cb:512 * cb + 512],
                                    start=(kc == 0), stop=(kc == KC - 1)),
                                    reads=[bwgv, bhT[ms]], writes=[bpv], inc=(kc == KC - 1))
                            evac(gvb[:, 512 * cb:512 * cb + 512], pv[:, :], [bpv], [bgvb])
                        for h in range(4):
                            pu = pbank[3]; bpu = bpb[3]
                            hh = h % 2
                            S.op("pe", lambda e, h=h, hh=hh: e.matmul(pu[:, 256 * hh:256 * hh + 256], lhsT=khat[:, 128 * h:128 * h + 128],
                                                                      rhs=gvb[:, 256 * h:256 * h + 256], start=True, stop=True),
                                 reads=[bkhat, bgvb], writes=[bpu])
                            S.op("dve", lambda e, h=h, hh=hh: e.scalar_tensor_tensor(
                                out=Sst[:, 256 * h:256 * h + 256], in0=Sst[:, 256 * h:256 * h + 256], scalar=dec[:, h:h + 1],
                                in1=pu[:, 256 * hh:256 * hh + 256], op0=ALU.mult, op1=ALU.add),
                                reads=[bS, bdec, bpu], writes=[bS])
                        if t in snap_tiles:
                            cc = snap_tiles[t]
                            S.op("dve", lambda e, cc=cc: e.scalar_tensor_tensor(
                                out=Sown[:], in0=Sst[:], scalar=selt[:, cc:cc + 1], in1=Sown[:], op0=ALU.mult, op1=ALU.add),
                                reads=[bS, bsel, bSo], writes=[bSo])
                S.dma("sp", SOWN, Sown[:], bSo, reads=[bSo], writes=[bSOWN])
                if debug:
                    d1 = dout("dbg_KT", [8, 128, S_ALL], BF16)
                    d2 = dout("dbg_VV", [S_ALL, 1024], BF16)
                    d3 = dout("dbg_SOWN", [128, 1024])
                    bd = S.buf("dbgA")
                    S.barrier()
                    S.dma("sp", d1, KT, bd, reads=[bKT])
                    S.dma("sp", d2, VV, bd, reads=[bVV])
                    S.dma("sp", d3, SOWN, bd, reads=[bSOWN])
                S.barrier()
                print("stage A sbuf remaining", nc.sbuf_bytes_remaining, "insts", S.ninst)
                S.emit()

        own_macros = [list(range(i, min(i + 2, NOWN))) for i in range(0, NOWN, 2)]
        if "B" in stages:
            with ExitStack() as st:
                sb = lambda n, s, d: st.enter_context(nc.sbuf_tensor("B_" + n, list(s), d))
                wgq = sb("wgq", [128, KC, 512], BF16); bwgq = S.buf("wgq")
                wgk = sb("wgk", [128, KC, 512], BF16); bwgk = S.buf("wgk")
                wgv = sb("wgv", [128, KC, 1024], BF16); bwgv = S.buf("wgv")
                wgg = sb("wgg", [128, KC, 1024], BF16); bwgg = S.buf("wgg")
                wga = sb("wga", [128, KC, 16], BF16); bwga = S.buf("wga")
                wdq = sb("wdq", [128, KC, 1024], BF16); bwdq = S.buf("wdq")
                gbc = sb("gbc", [128, D], F32); bgbc = S.buf("gbc")
                wal = sb("wal", [17, 512], F32); bwal = S.buf("wal")
                ident = sb("ident", [128, 128], BF16); bid = S.buf("ident")
                tri = sb("tri", [128, 4, 128], F32); btri = S.buf("tri")
                mask4 = sb("mask4", [128, 4, 128], F32); bm4 = S.buf("mask4")
                flagt = sb("flagt", [128, 1], F32); bflag = S.buf("flagt")
                glab = sb("glab", [128, 256], F32); bglab = S.buf("glab")
                xs = sb("xs", [128, D], F32); bxs = S.buf("xs")
                ssq = sb("ssq", [128, 2], F32); bssq = S.buf("ssq")
                hb = sb("hb", [128, D], BF16); bhb = S.buf("hb")
                hT = sb("hT", [128, KC, 256], BF16); bhT = S.buf("hT")
                qTf = sb("qTf", [128, 4, 256], F32); bqTf = S.buf("qTf")
                kTf = sb("kTf", [128, 4, 256], F32); bkTf = S.buf("kTf")
                qst = [sb("qst%d" % i, [128, 256], BF16) for i in range(2)]; bqst = [S.buf("qst%d" % i) for i in range(2)]
                gaT = sb("gaT", [17, 256], F32); bgaT = S.buf("gaT")
                e1 = sb("e1", [128, 512], F32); be1 = S.buf("e1")
                spt = sb("spt", [128, 512], F32); bspt = S.buf("spt")
                E3 = sb("E3", [128, 512], F32); bE3 = S.buf("E3")
                E1T = sb("E1T", [128, 512], F32); bE1T = S.buf("E1T")
                E2T = sb("E2T", [128, 512], F32); bE2T = S.buf("E2T")
                qtl = sb("qtl", [128, 4, 128], BF16); bqtl = S.buf("qtl")
                ktl = sb("ktl", [128, 4, 128], BF16); bktl = S.buf("ktl")
                AT = sb("AT", [128, 4, 128], BF16); bAT = S.buf("AT")
                khat = sb("khat", [128, 512], BF16); bkhat = S.buf("khat")
                gvb = sb("gvb", [128, 1024], BF16); bgvb = S.buf("gvb")
                gate = sb("gate", [128, 1024], F32); bgate = S.buf("gate")
                Sst = sb("Sst", [128, 1024], F32); bS = S.buf("Sst")
                Sbb = sb("Sbb", [128, 1024], BF16); bSb = S.buf("Sbb")
                oss = sb("oss", [128, 8], F32); boss = S.buf("oss")
                omA = [sb("omA%d" % i, [128, 1024], BF16) for i in range(2)]; bomA = [S.buf("omA%d" % i) for i in range(2)]
                bpb = [S.buf("pbB%d" % i) for i in range(8)]

                S.dma("sp", ident[:], ident_d, bid, writes=[bid])
                S.dma("sp", tri[:], tri_d.rearrange("a p n -> p a n"), btri, writes=[btri])
                for h in range(4):
                    S.dma("sp", mask4[:, h, :], tri_d[2], bm4, writes=[bm4])
                S.dma("sp", flagt[:], flag_d, bflag, writes=[bflag])
                S.dma("sp", wal[:], walpha, bwal, writes=[bwal])
                S.dma("sp", gbc[:], nrm[0:1, :].partition_broadcast(128), bgbc, writes=[bgbc])
                S.dma("sp", glab[:], hnrm[0:1, :].partition_broadcast(128), bglab, writes=[bglab])
                S.dma("sp", Sst[:], SOWN, bS, reads=[bSOWN], writes=[bS])
                S.op("pool", lambda e: e.memset(gaT[:], 1.0), writes=[bgaT])
                for (wt, bw, c0, ncol) in ((wgq, bwgq, C_GQ, 512), (wgk, bwgk, C_GK, 512), (wgv, bwgv, C_GV, 1024),
                                           (wgg, bwgg, C_GG, 1024), (wdq, bwdq, C_DQ, 1024)):
                    for kc in range(KC):
                        S.dma("pool", wt[:, kc, :], w_view(w_in, c0, ncol)[:, kc, :], bw, writes=[bw])
                S.dma("pool", wga[:], w_view(w_in, C_GA, 16), bwga, writes=[bwga])
                S.op("act", lambda e: e.activation(out=Sbb[:], in_=Sst[:], func=AF.Copy), reads=[bS], writes=[bSb])

                ptr_bf = [pbank[0][:].bitcast(BF16), pbank[1][:].bitcast(BF16)]
                tg = [0]

                def evacB(out, in_, reads, writes, eng=None):
                    if eng is None:
                        eng = ("act", "dve")[tg[0] % 2]
                        tg[0] += 1
                    if eng == "act":
                        return S.op("act", lambda e: e.activation(out=out, in_=in_, func=AF.Copy), reads=reads, writes=writes)
                    return S.op("dve", lambda e: e.tensor_copy(out=out, in_=in_), reads=reads, writes=writes)

                def proj_tm(wt, bw, j, cb, pbk):
                    for kc in range(KC):
                        S.op("pe", lambda e, kc=kc: e.matmul(pbank[pbk][:, :], lhsT=hT[:, kc, 128 * j:128 * j + 128],
                                                             rhs=wt[:, kc, 512 * cb:512 * cb + 512],
                                                             start=(kc == 0), stop=(kc == KC - 1)),
                             reads=[bw, bhT], writes=[bpb[pbk]], inc=(kc == KC - 1))

                omi_c = [0]

                def do_macro_B(tiles):
                    N = 128 * len(tiles)
                    for j, li in enumerate(tiles):
                        S.dma("sp", xs[:], x_own[128 * li:128 * li + 128, :], bxs, writes=[bxs])
                        S.op("pool", lambda e: e.memset(ssq[:], 0.0), writes=[bssq])
                        S.op("act", lambda e: e.activation(out=hb[:], in_=xs[:], func=AF.Square, accum_out=ssq[:, 0:1]),
                             reads=[bxs], writes=[bhb, bssq])
                        rstd_ops(ssq, bssq, D)
                        S.op("dve", lambda e: e.scalar_tensor_tensor(out=hb[:], in0=xs[:], scalar=ssq[:, 1:2], in1=gbc[:],
                                                                    op0=ALU.mult, op1=ALU.mult),
                             reads=[bxs, bssq, bgbc], writes=[bhb])
                        for half in range(2):
                            pt = ptr_bf[half]
                            for k8 in range(8):
                                kc = half * 8 + k8
                                S.op("pe", lambda e, pt=pt, k8=k8, kc=kc: e.transpose(
                                    out=pt[:, 128 * k8:128 * k8 + 128], in_=hb[:, 128 * kc:128 * kc + 128], identity=ident[:]),
                                    reads=[bhb, bid], writes=[bpb[half]], inc=(k8 == 7))
                            evacB(hT[:, half * 8:half * 8 + 8, 128 * j:128 * j + 128],
                                  pt.rearrange("p (k n) -> p k n", k=8), [bpb[half]], [bhT])
                    for (wt, bw, dst, bdst) in ((wgq, bwgq, qTf, bqTf), (wgk, bwgk, kTf, bkTf)):
                        for cc in range(4):
                            pk = 2 + cc % 2
                            for kc in range(KC):
                                S.op("pe", lambda e, wt=wt, pk=pk, cc=cc, kc=kc: e.matmul(
                                    pbank[pk][:, 0:N], lhsT=wt[:, kc, 128 * cc:128 * cc + 128], rhs=hT[:, kc, 0:N],
                                    start=(kc == 0), stop=(kc == KC - 1)),
                                    reads=[bw, bhT], writes=[bpb[pk]], inc=(kc == KC - 1))
                            evacB(dst[:, cc, 0:N], pbank[pk][:, 0:N], [bpb[pk]], [bdst])
                    for cc in range(8):
                        pk = 2 + cc % 2
                        for kc in range(KC):
                            S.op("pe", lambda e, pk=pk, cc=cc, kc=kc: e.matmul(
                                pbank[pk][:, 0:N], lhsT=wdq[:, kc, 128 * cc:128 * cc + 128], rhs=hT[:, kc, 0:N],
                                start=(kc == 0), stop=(kc == KC - 1)),
                                reads=[bwdq, bhT], writes=[bpb[pk]], inc=(kc == KC - 1))
                        qs = cc % 2
                        S.op("act", lambda e, pk=pk, qs=qs: e.mul(out=qst[qs][:, 0:N], in_=pbank[pk][:, 0:N], mul=128.0 ** -0.5),
                             reads=[bpb[pk]], writes=[bqst[qs]])
                        S.dma("sp", QT[cc, :, 128 * tiles[0]:128 * tiles[0] + N], qst[qs][:, 0:N], bqst[qs],
                              reads=[bqst[qs]], writes=[bQT])
                    for kc in range(KC):
                        S.op("pe", lambda e, kc=kc: e.matmul(pbank[4][0:16, 0:N], lhsT=wga[:, kc, :], rhs=hT[:, kc, 0:N],
                                                             start=(kc == 0), stop=(kc == KC - 1)),
                             reads=[bwga, bhT], writes=[bpb[4]], inc=(kc == KC - 1))
                    evacB(gaT[0:16, 0:N], pbank[4][0:16, 0:N], [bpb[4]], [bgaT], eng="dve")
                    for j, li in enumerate(tiles):
                        js = slice(128 * j, 128 * j + 128)
                        S.op("pe", lambda e, js=js: e.matmul(pbank[4][:, :], lhsT=gaT[:, js], rhs=wal[:, :], start=True, stop=True),
                             reads=[bgaT, bwal], writes=[bpb[4]])
                        S.op("act", lambda e: e.activation(out=e1[:], in_=pbank[4][:, :], func=AF.Exp, scale=-1.0),
                             reads=[bpb[4]], writes=[be1])
                        S.op("act", lambda e: e.activation(out=spt[:], in_=e1[:], func=AF.Ln, bias=epsc[:, 1:2]),
                             reads=[be1, bepsc], writes=[bspt])
                        S.op("pe", lambda e: e.matmul(pbank[4][:, :], lhsT=tri[:, 0, :], rhs=spt[:], start=True, stop=True),
                             reads=[btri, bspt], writes=[bpb[4]])
                        S.op("act", lambda e: e.activation(out=E3[:], in_=pbank[4][:, :], func=AF.Exp),
                             reads=[bpb[4]], writes=[bE3])
                        for h in range(4):
                            S.op("pe", lambda e, h=h: e.matmul(pbank[5][:, 128 * h:128 * h + 128], lhsT=spt[:, 128 * h:128 * h + 128],
                                                               rhs=tri[:, 1, :], start=True, stop=True),
                                 reads=[bspt, btri], writes=[bpb[5]], inc=(h == 3))
                        S.op("act", lambda e: e.activation(out=E1T[:], in_=pbank[5][:, :], func=AF.Exp),
                             reads=[bpb[5]], writes=[bE1T])
                        S.op("act", lambda e: e.activation(out=E2T[:], in_=pbank[5][:, :], func=AF.Exp, scale=-1.0),
                             reads=[bpb[5]], writes=[bE2T])
                        S.op("dve", lambda e, js=js: e.scalar_tensor_tensor(
                            out=qtl[:], in0=qTf[:, :, js], scalar=128.0 ** -0.5, in1=E1T[:].rearrange("p (h n) -> p h n", h=4),
                            op0=ALU.mult, op1=ALU.mult), reads=[bqTf, bE1T], writes=[bqtl])
                        S.op("dve", lambda e, js=js: e.tensor_tensor(
                            out=ktl[:], in0=kTf[:, :, js], in1=E2T[:].rearrange("p (h n) -> p h n", h=4), op=ALU.mult),
                            reads=[bkTf, bE2T], writes=[bktl])
                        proj_tm(wgk, bwgk, j, 0, 2)
                        S.op("dve", lambda e: e.tensor_tensor(out=khat[:], in0=pbank[2][:, :], in1=E3[:], op=ALU.mult),
                             reads=[bpb[2], bE3], writes=[bkhat])
                        for cb in range(2):
                            proj_tm(wgv, bwgv, j, cb, 2 + cb)
                            evacB(gvb[:, 512 * cb:512 * cb + 512], pbank[2 + cb][:, :], [bpb[2 + cb]], [bgvb])
                        for cb in range(2):
                            proj_tm(wgg, bwgg, j, cb, 2 + cb)
                            S.op("act", lambda e, cb=cb: e.activation(out=gate[:, 512 * cb:512 * cb + 512], in_=pbank[2 + cb][:, :],
                                                                      func=AF.Silu),
                                 reads=[bpb[2 + cb]], writes=[bgate])
                        for h in range(4):
                            S.op("pe", lambda e, h=h: e.matmul(pbank[4][:, 128 * h:128 * h + 128], lhsT=ktl[:, h, :], rhs=qtl[:, h, :],
                                                               start=True, stop=True),
                                 reads=[bktl, bqtl], writes=[bpb[4]], inc=(h == 3))
                        S.op("dve", lambda e: e.tensor_tensor(out=AT[:], in0=pbank[4][:, :].rearrange("p (h n) -> p h n", h=4),
                                                              in1=mask4[:], op=ALU.mult),
                             reads=[bpb[4], bm4], writes=[bAT])
                        for h in range(4):
                            ob = 6 + h // 2
                            osl = slice(256 * (h % 2), 256 * (h % 2) + 256)
                            S.op("pe", lambda e, h=h, ob=ob, osl=osl: e.matmul(pbank[ob][:, osl], lhsT=AT[:, h, :],
                                                                               rhs=gvb[:, 256 * h:256 * h + 256], start=True, stop=False),
                                 reads=[bAT, bgvb], writes=[bpb[ob]], inc=False)
                            S.op("pe", lambda e, h=h, ob=ob, osl=osl: e.matmul(pbank[ob][:, osl], lhsT=qtl[:, h, :],
                                                                               rhs=Sbb[:, 256 * h:256 * h + 256], start=False, stop=True),
                                 reads=[bqtl, bSb], writes=[bpb[ob]])
                        for h in range(4):
                            usl = slice(256 * (h % 2), 256 * (h % 2) + 256)
                            S.op("pe", lambda e, h=h, usl=usl: e.matmul(pbank[5][:, usl], lhsT=khat[:, 128 * h:128 * h + 128],
                                                                        rhs=gvb[:, 256 * h:256 * h + 256], start=True, stop=True),
                                 reads=[bkhat, bgvb], writes=[bpb[5]])
                            S.op("dve", lambda e, h=h, usl=usl: e.scalar_tensor_tensor(
                                out=Sst[:, 256 * h:256 * h + 256], in0=Sst[:, 256 * h:256 * h + 256],
                                scalar=E1T[:, 128 * h + 127:128 * h + 128], in1=pbank[5][:, usl], op0=ALU.mult, op1=ALU.add),
                                reads=[bS, bE1T, bpb[5]], writes=[bS])
                        if li == 0:
                            S.op("dve", lambda e: e.tensor_scalar_mul(out=Sst[:], in0=Sst[:], scalar1=flagt[:, 0:1]),
                                 reads=[bS, bflag], writes=[bS])
                        S.op("act", lambda e: e.activation(out=Sbb[:], in_=Sst[:], func=AF.Copy), reads=[bS], writes=[bSb])
                        S.op("pool", lambda e: e.memset(oss[:], 0.0), writes=[boss])
                        om = omA[omi_c[0] % 2]; bom = bomA[omi_c[0] % 2]; omi_c[0] += 1
                        for h in range(4):
                            ob = 6 + h // 2
                            osl = slice(256 * (h % 2), 256 * (h % 2) + 256)
                            S.op("act", lambda e, h=h, ob=ob, osl=osl, om=om: e.activation(
                                out=om[:, 256 * h:256 * h + 256], in_=pbank[ob][:, osl], func=AF.Square, accum_out=oss[:, h:h + 1]),
                                reads=[bpb[ob]], writes=[bom, boss])
                            S.op("dve", lambda e, h=h: e.tensor_tensor(out=gate[:, 256 * h:256 * h + 256], in0=gate[:, 256 * h:256 * h + 256],
                                                                       in1=glab[:], op=ALU.mult), reads=[bgate, bglab], writes=[bgate])
                        S.op("act", lambda e: e.activation(out=oss[:, 4:8], in_=oss[:, 0:4], func=AF.Ln, scale=1.0 / 256, bias=epsc[:, 0:1]),
                             reads=[boss, bepsc], writes=[boss])
                        S.op("act", lambda e: e.activation(out=oss[:, 4:8], in_=oss[:, 4:8], func=AF.Exp, scale=-0.5),
                             reads=[boss], writes=[boss])
                        for h in range(4):
                            ob = 6 + h // 2
                            osl = slice(256 * (h % 2), 256 * (h % 2) + 256)
                            S.op("dve", lambda e, h=h, ob=ob, osl=osl, om=om: e.scalar_tensor_tensor(
                                out=om[:, 256 * h:256 * h + 256], in0=pbank[ob][:, osl], scalar=oss[:, 4 + h:5 + h],
                                in1=gate[:, 256 * h:256 * h + 256], op0=ALU.mult, op1=ALU.mult),
                                reads=[bpb[ob], boss, bgate], writes=[bom])
                        S.dma("sp", OM[128 * li:128 * li + 128, 0:1024], om[:], bom, reads=[bom], writes=[bOM])
                for tiles_ in own_macros:
                    do_macro_B(tiles_)
                if debug:
                    dq_ = dout("dbg_QT", [8, 128, S_OWN], BF16)
                    do_ = dout("dbg_OMA", [S_OWN, D], BF16)
                    bd = S.buf("dbgB")
                    S.barrier()
                    S.dma("sp", dq_, QT, bd, reads=[bQT])
                    S.dma("sp", do_, OM, bd, reads=[bOM])
                S.barrier()
                print("stage B sbuf remaining", nc.sbuf_bytes_remaining, "insts", S.ninst)
                S.emit()

        if "C" in stages:
            groups = [[0]] + [[2 * g - 1, 2 * g] for g in range(1, NG)]
            PK = min(16, NT)
            NP = NT // PK
            with ExitStack() as st:
                sb = lambda n, s, d: st.enter_context(nc.sbuf_tensor("C_" + n, list(s), d))
                KTs = sb("KTs", [128, 2, S_ALL], BF16)
                bK = [[S.buf("K%d_%d" % (c_, p_)) for p_ in range(NP)] for c_ in range(2)]
                Vs = sb("Vs", [128, NT, 257], BF16)
                bV = [S.buf("V%d" % p_) for p_ in range(NP)]
                posr = sb("posr", [128, NG * NT], F32); bposr = S.buf("posr")
                biasH = sb("biasH", [128, NG * NT], F32); bbias = S.buf("biasH")
                mkp = sb("mkp", [128, 16, 256], BF16); bmkp = S.buf("mkp")
                mkh = sb("mkh", [128, 8, 128], BF16); bmkh = S.buf("mkh")
                QTs = [sb("QTs%d" % i, [128, 2, 256], BF16) for i in range(2)]; bQTs = [S.buf("QTs%d" % i) for i in range(2)]
                PT = [sb("PT%d" % i, [128, 256], BF16) for i in range(4)]; bPT = [S.buf("PT%d" % i) for i in range(4)]
                lv = sb("lv", [1, 4, 128], F32); blv = S.buf("lv")
                lpr = sb("lpr", [1, 2, 128], F32); blpr = S.buf("lpr")
                ls = sb("ls", [1, 4], F32); bls = S.buf("ls")
                ones1 = sb("ones1", [1, 128], F32); bones = S.buf("ones1")
                lamc = sb("lamc", [128, 1], F32); blamc = S.buf("lamc")
                dnbc = sb("dnbc", [128, 256], F32); bdn = S.buf("dnbc")
                rr = sb("rr", [128, 4], F32); brr = S.buf("rr")
                t2 = sb("t2", [128, 256], F32); bt2 = S.buf("t2")
                of = sb("of", [128, 256], F32); bof = S.buf("of")
                osq = sb("osq", [128, 256], F32); bosq = S.buf("osq")
                ss2 = sb("ss2", [128, 2], F32); bss2 = S.buf("ss2")
                obf = [sb("obf%d" % i, [128, 256], BF16) for i in range(2)]; bobf = [S.buf("obf%d" % i) for i in range(2)]
                bpb = [S.buf("pbC%d" % i) for i in range(8)]
                bst = [S.buf("stC%d" % i) for i in range(4)]

                S.dma("sp", posr[:], posrel_d, bposr, writes=[bposr])
                S.dma("sp", mkp[:], maskp_d, bmkp, writes=[bmkp])
                S.dma("sp", mkh[:], maskh_d, bmkh, writes=[bmkh])
                S.dma("sp", lv[:], lamv.rearrange("(o a) n -> o a n", o=1), blv, writes=[blv])
                S.dma("sp", dnbc[:], hnrm[1:2, :].partition_broadcast(128), bdn, writes=[bdn])
                S.op("act", lambda e: e.mul(out=dnbc[:], in_=dnbc[:], mul=1.0 - LAMBDA_INIT), reads=[bdn], writes=[bdn])
                S.op("pool", lambda e: e.memset(ones1[:], 1.0), writes=[bones])
                S.op("pool", lambda e: e.memset(Vs[:, :, 256:257], 1.0), writes=bV)
                S.op("dve", lambda e: e.tensor_tensor(out=lpr[:, 0, :], in0=lv[:, 0, :], in1=lv[:, 1, :], op=ALU.mult),
                     reads=[blv], writes=[blpr])
                S.op("dve", lambda e: e.tensor_tensor(out=lpr[:, 1, :], in0=lv[:, 2, :], in1=lv[:, 3, :], op=ALU.mult),
                     reads=[blv], writes=[blpr])
                S.op("dve", lambda e: e.reduce_sum(out=ls[:, 0:2], in_=lpr[:], axis=mybir.AxisListType.X),
                     reads=[blpr], writes=[bls])
                S.op("act", lambda e: e.activation(out=ls[:, 2:4], in_=ls[:, 0:2], func=AF.Exp), reads=[bls], writes=[bls])
                S.op("dve", lambda e: e.tensor_tensor(out=ls[:, 0:1], in0=ls[:, 2:3], in1=ls[:, 3:4], op=ALU.subtract),
                     reads=[bls], writes=[bls])
                S.op("dve", lambda e: e.tensor_scalar_add(out=ls[:, 0:1], in0=ls[:, 0:1], scalar1=LAMBDA_INIT),
                     reads=[bls], writes=[bls])
                S.op("pe", lambda e: e.matmul(pbank[7][:, 0:1], lhsT=ones1[:], rhs=ls[:, 0:1], start=True, stop=True),
                     reads=[bones, bls], writes=[bpb[7]])
                S.op("dve", lambda e: e.tensor_copy(out=lamc[:], in_=pbank[7][:, 0:1]), reads=[bpb[7]], writes=[blamc])

                qsl = 0
                ptc = 0
                stc = 0
                obc = 0
                for h in range(4):
                    for comp in range(2):
                        for p_ in range(NP):
                            S.dma("sp", KTs[:, comp, 128 * PK * p_:128 * PK * (p_ + 1)],
                                  KT[2 * h + comp, :, 128 * PK * p_:128 * PK * (p_ + 1)], bK[comp][p_],
                                  reads=[bKT], writes=[bK[comp][p_]])
                    for p_ in range(NP):
                        S.dma("sp", Vs[:, PK * p_:PK * (p_ + 1), 0:256],
                              VV[128 * PK * p_:128 * PK * (p_ + 1), 256 * h:256 * h + 256].rearrange("(t p) e -> p t e", p=128),
                              bV[p_], reads=[bVV], writes=[bV[p_]])
                    S.op("dve", lambda e, h=h: e.tensor_scalar_mul(out=biasH[:], in0=posr[:], scalar1=float(SLOPES[h])),
                         reads=[bposr], writes=[bbias])
                    for g, tl in enumerate(groups):
                        nq = 128 * len(tl)
                        q0 = 128 * tl[0]
                        qs = qsl % 2; qsl += 1
                        for comp in range(2):
                            S.dma("sp", QTs[qs][:, comp, 0:nq], QT[2 * h + comp, :, q0:q0 + nq], bQTs[qs],
                                  reads=[bQT], writes=[bQTs[qs]])
                        if g == 0:
                            cand = {0: 0}
                            for cp in range(1, 8):
                                cand[nb * cp - 1] = cp
                        else:
                            cand = {}
                            for cp in range(8):
                                for ab in range(2):
                                    cand[nb * cp + 2 * g - 2 + ab] = 2 * cp + ab
                        for kt in range(NT):
                            p_ = kt // PK
                            for comp in range(2):
                                sslot = stc % 4; stc += 1
                                sps = pbank[4 + sslot // 2][:, 256 * (sslot % 2):256 * (sslot % 2) + nq]
                                S.op("pe", lambda e, sps=sps, comp=comp, kt=kt, qs=qs, nq=nq: e.matmul(
                                    sps, lhsT=KTs[:, comp, 128 * kt:128 * kt + 128], rhs=QTs[qs][:, comp, 0:nq], start=True, stop=True),
                                    reads=[bK[comp][p_], bQTs[qs]], writes=[bst[sslot]])
                                pi = ptc % 4; ptc += 1
                                bidx = g * NT + kt
                                S.op("act", lambda e, sps=sps, pi=pi, nq=nq, bidx=bidx: e.activation(
                                    out=PT[pi][:, 0:nq], in_=sps, func=AF.Exp, bias=biasH[:, bidx:bidx + 1]),
                                    reads=[bst[sslot], bbias], writes=[bPT[pi]])
                                if kt in cand:
                                    mk = mkh[:, cand[kt], 0:nq] if g == 0 else mkp[:, cand[kt], 0:nq]
                                    bmk = bmkh if g == 0 else bmkp
                                    S.op("dve", lambda e, pi=pi, nq=nq, mk=mk: e.tensor_tensor(
                                        out=PT[pi][:, 0:nq], in0=PT[pi][:, 0:nq], in1=mk, op=ALU.mult),
                                        reads=[bPT[pi], bmk], writes=[bPT[pi]])
                                for ti in range(len(tl)):
                                    ob = 2 * ti + comp
                                    S.op("pe", lambda e, ob=ob, pi=pi, ti=ti, kt=kt: e.matmul(
                                        pbank[ob][:, 0:257], lhsT=PT[pi][:, 128 * ti:128 * ti + 128], rhs=Vs[:, kt, :],
                                        start=(kt == 0), stop=(kt == NT - 1)),
                                        reads=[bPT[pi], bV[p_]], writes=[bpb[ob]], inc=(ti == len(tl) - 1))
                        for ti, li in enumerate(tl):
                            O1 = pbank[2 * ti]; O2 = pbank[2 * ti + 1]
                            b1 = bpb[2 * ti]; b2 = bpb[2 * ti + 1]
                            S.op("dve", lambda e, O1=O1: e.reciprocal(out=rr[:, 0:1], in_=O1[:, 256:257]), reads=[b1], writes=[brr])
                            S.op("dve", lambda e, O2=O2: e.reciprocal(out=rr[:, 1:2], in_=O2[:, 256:257]), reads=[b2], writes=[brr])
                            S.op("dve", lambda e: e.tensor_tensor(out=rr[:, 2:3], in0=rr[:, 1:2], in1=lamc[:], op=ALU.mult),
                                 reads=[brr, blamc], writes=[brr])
                            S.op("dve", lambda e, O2=O2: e.tensor_scalar_mul(out=t2[:], in0=O2[:, 0:256], scalar1=rr[:, 2:3]),
                                 reads=[b2, brr], writes=[bt2])
                            S.op("dve", lambda e, O1=O1: e.scalar_tensor_tensor(out=of[:], in0=O1[:, 0:256], scalar=rr[:, 0:1], in1=t2[:],
                                                                               op0=ALU.mult, op1=ALU.subtract),
                                 reads=[b1, brr, bt2], writes=[bof])
                            S.op("pool", lambda e: e.memset(ss2[:], 0.0), writes=[bss2])
                            S.op("act", lambda e: e.activation(out=osq[:], in_=of[:], func=AF.Square, accum_out=ss2[:, 0:1]),
                                 reads=[bof], writes=[bosq, bss2])
                            rstd_ops(ss2, bss2, 256)
                            oi = obc % 2; obc += 1
                            S.op("dve", lambda e, oi=oi: e.scalar_tensor_tensor(out=obf[oi][:], in0=of[:], scalar=ss2[:, 1:2], in1=dnbc[:],
                                                                               op0=ALU.mult, op1=ALU.mult),
                                 reads=[bof, bss2, bdn], writes=[bobf[oi]])
                            S.dma("sp", OM[128 * li:128 * li + 128, 1024 + 256 * h:1024 + 256 * h + 256], obf[oi][:], bobf[oi],
                                  reads=[bobf[oi]], writes=[bOM])
                if debug:
                    do2 = dout("dbg_OM", [S_OWN, D], BF16)
                    bd = S.buf("dbgC")
                    S.barrier()
                    S.dma("sp", do2, OM, bd, reads=[bOM])
                S.barrier()
                print("stage C sbuf remaining", nc.sbuf_bytes_remaining, "insts", S.ninst)
                S.emit()

        if "D" in stages:
            with ExitStack() as stDE:
                h2T = stDE.enter_context(nc.sbuf_tensor("h2T", [128, KC, S_OWN], BF16)); bh2T = S.buf("h2T")
                identE = stDE.enter_context(nc.sbuf_tensor("identE", [128, 128], BF16)); bidE = S.buf("identE")
                S.dma("sp", identE[:], ident_d, bidE, writes=[bidE])
                with ExitStack() as st:
                    sb = lambda n, s, d: st.enter_context(nc.sbuf_tensor("D_" + n, list(s), d))
                    wo = sb("wo", [128, KC, D], BF16); bwo = S.buf("wo")
                    g1bc = sb("g1bc", [128, D], F32); bg1 = S.buf("g1bc")
                    g2bc = sb("g2bc", [128, D], F32); bg2 = S.buf("g2bc")
                    om = [sb("om%d" % i, [128, D], BF16) for i in range(2)]; bom = [S.buf("om%d" % i) for i in range(2)]
                    omT = sb("omT", [128, KC, 128], BF16); bomT = S.buf("omT")
                    xs = sb("xs", [128, D], F32); bxs = S.buf("xs")
                    x1 = sb("x1", [128, D], F32); bx1 = S.buf("x1")
                    tmp = sb("tmp", [128, D], F32); btmp = S.buf("tmp")
                    h2 = sb("h2", [128, D], BF16); bh2 = S.buf("h2")
                    s4 = sb("s4", [128, 8], F32); bs4 = S.buf("s4")
                    bpb = [S.buf("pbD%d" % i) for i in range(8)]
                    S.dma("sp", g1bc[:], nrm[1:2, :].partition_broadcast(128), bg1, writes=[bg1])
                    S.dma("sp", g2bc[:], nrm[2:3, :].partition_broadcast(128), bg2, writes=[bg2])
                    for kc in range(KC):
                        S.dma("pool", wo[:, kc, :], w_o.rearrange("(c p) n -> p c n", p=128)[:, kc, :], bwo, writes=[bwo])
                    ptr_bf = [pbank[0][:].bitcast(BF16), pbank[1][:].bitcast(BF16)]
                    tgd = [0]

                    def evacD(out, in_, reads, writes):
                        eng = ("act", "dve")[tgd[0] % 2]
                        tgd[0] += 1
                        if eng == "act":
                            return S.op("act", lambda e: e.activation(out=out, in_=in_, func=AF.Copy), reads=reads, writes=writes)
                        return S.op("dve", lambda e: e.tensor_copy(out=out, in_=in_), reads=reads, writes=writes)

                    for li in range(NOWN):
                        o_ = om[li % 2]; bo_ = bom[li % 2]
                        S.dma("sp", o_[:], OM[128 * li:128 * li + 128, :], bo_, reads=[bOM], writes=[bo_])
                        S.dma("sp", xs[:], x_own[128 * li:128 * li + 128, :], bxs, writes=[bxs])
                        for half in range(2):
                            pt = ptr_bf[half]
                            for k8 in range(8):
                                kc = half * 8 + k8
                                S.op("pe", lambda e, pt=pt, k8=k8, kc=kc, o_=o_: e.transpose(
                                    out=pt[:, 128 * k8:128 * k8 + 128], in_=o_[:, 128 * kc:128 * kc + 128], identity=identE[:]),
                                    reads=[bo_, bidE], writes=[bpb[half]], inc=(k8 == 7))
                            evacD(omT[:, half * 8:half * 8 + 8, :], pt.rearrange("p (k n) -> p k n", k=8), [bpb[half]], [bomT])
                        for cb in range(4):
                            for kc in range(KC):
                                S.op("pe", lambda e, cb=cb, kc=kc: e.matmul(pbank[4 + cb][:, :], lhsT=omT[:, kc, :],
                                                                            rhs=wo[:, kc, 512 * cb:512 * cb + 512],
                                                                            start=(kc == 0), stop=(kc == KC - 1)),
                                     reads=[bomT, bwo], writes=[bpb[4 + cb]], inc=(kc == KC - 1))
                        S.op("pool", lambda e: e.memset(s4[:], 0.0), writes=[bs4])
                        for cb in range(4):
                            S.op("act", lambda e, cb=cb: e.activation(out=tmp[:, 512 * cb:512 * cb + 512], in_=pbank[4 + cb][:, :],
                                                                      func=AF.Square, accum_out=s4[:, cb:cb + 1]),
                                 reads=[bpb[4 + cb]], writes=[btmp, bs4])
                        S.op("dve", lambda e: e.reduce_sum(out=s4[:, 4:5], in_=s4[:, 0:4], axis=mybir.AxisListType.X),
                             reads=[bs4], writes=[bs4])
                        S.op("act", lambda e: e.activation(out=s4[:, 5:6], in_=s4[:, 4:5], func=AF.Ln, scale=1.0 / D, bias=epsc[:, 0:1]),
                             reads=[bs4, bepsc], writes=[bs4])
                        S.op("act", lambda e: e.activation(out=s4[:, 5:6], in_=s4[:, 5:6], func=AF.Exp, scale=-0.5),
                             reads=[bs4], writes=[bs4])
                        for cb in range(4):
                            S.op("dve", lambda e, cb=cb: e.scalar_tensor_tensor(
                                out=tmp[:, 512 * cb:512 * cb + 512], in0=pbank[4 + cb][:, :], scalar=s4[:, 5:6],
                                in1=g1bc[:, 512 * cb:512 * cb + 512], op0=ALU.mult, op1=ALU.mult),
                                reads=[bpb[4 + cb], bs4, bg1], writes=[btmp])
                        S.op("dve", lambda e: e.tensor_tensor(out=x1[:], in0=tmp[:], in1=xs[:], op=ALU.add),
                             reads=[btmp, bxs], writes=[bx1])
                        S.dma("sp", X1[128 * li:128 * li + 128, :], x1[:], bx1, reads=[bx1], writes=[bX1])
                        S.op("pool", lambda e: e.memset(s4[:, 6:8], 0.0), writes=[bs4])
                        S.op("act", lambda e: e.activation(out=h2[:], in_=x1[:], func=AF.Square, accum_out=s4[:, 6:7]),
                             reads=[bx1], writes=[bh2, bs4])
                        S.op("act", lambda e: e.activation(out=s4[:, 7:8], in_=s4[:, 6:7], func=AF.Ln, scale=1.0 / D, bias=epsc[:, 0:1]),
                             reads=[bs4, bepsc], writes=[bs4])
                        S.op("act", lambda e: e.activation(out=s4[:, 7:8], in_=s4[:, 7:8], func=AF.Exp, scale=-0.5),
                             reads=[bs4], writes=[bs4])
                        S.op("dve", lambda e: e.scalar_tensor_tensor(out=h2[:], in0=x1[:], scalar=s4[:, 7:8], in1=g2bc[:],
                                                                    op0=ALU.mult, op1=ALU.mult),
                             reads=[bx1, bs4, bg2], writes=[bh2])
                        for half in range(2):
                            pt = ptr_bf[half]
                            for k8 in range(8):
                                kc = half * 8 + k8
                                S.op("pe", lambda e, pt=pt, k8=k8, kc=kc: e.transpose(
                                    out=pt[:, 128 * k8:128 * k8 + 128], in_=h2[:, 128 * kc:128 * kc + 128], identity=identE[:]),
                                    reads=[bh2, bidE], writes=[bpb[half]], inc=(k8 == 7))
                            evacD(h2T[:, half * 8:half * 8 + 8, 128 * li:128 * li + 128],
                                  pt.rearrange("p (k n) -> p k n", k=8), [bpb[half]], [bh2T])
                    if debug:
                        dx1 = dout("dbg_X1", [S_OWN, D], F32)
                        bd = S.buf("dbgD")
                        S.barrier()
                        S.dma("sp", dx1, X1, bd, reads=[bX1])
                    S.barrier()
                    print("stage D sbuf remaining", nc.sbuf_bytes_remaining, "insts", S.ninst)
                    S.emit()

                with ExitStack() as st:
                  if "E" in stages:
                      sb = lambda n, s, d: st.enter_context(nc.sbuf_tensor("E_" + n, list(s), d))
                      NW1 = 3
                      W1s = [sb("W1s%d" % i, [128, KC, 256], BF16) for i in range(NW1)]; bW1 = [S.buf("W1s%d" % i) for i in range(NW1)]
                      NW2 = 3
                      W2s = [sb("W2s%d" % i, [128, 1024], BF16) for i in range(NW2)]; bW2 = [S.buf("W2s%d" % i) for i in range(NW2)]
                      gT = sb("gT", [128, NFF, 512], BF16); bgT = S.buf("gT")
                      cvp = sb("cvp", [128, 4, NFF], F32); bcvp = S.buf("cvp")
                      flagt = sb("flagt", [128, 1], F32); bflag = S.buf("flagt")
                      g3bc = sb("g3bc", [128, D], F32); bg3 = S.buf("g3bc")
                      carry = sb("carry", [128, NFF, 2], F32); bcar = S.buf("carry")
                      cb_ = [sb("cbuf%d" % i, [128, 512], F32) for i in range(2)]; bcb = [S.buf("cbuf%d" % i) for i in range(2)]
                      gel = [sb("gel%d" % i, [128, 512], F32) for i in range(2)]; bgel = [S.buf("gel%d" % i) for i in range(2)]
                      fsave = sb("fsave", [128, 4, 2048], F32); bfs = S.buf("fsave")
                      x1t = [sb("x1t%d" % i, [128, D], F32) for i in range(2)]; bx1t = [S.buf("x1t%d" % i) for i in range(2)]
                      s8 = sb("s8", [128, 32], F32); bs8 = S.buf("s8")
                      bpb = [S.buf("pbE%d" % i) for i in range(8)]
                      S.dma("sp", cvp[:], convp, bcvp, writes=[bcvp])
                      S.dma("sp", flagt[:], flag_d, bflag, writes=[bflag])
                      S.dma("sp", g3bc[:], nrm[3:4, :].partition_broadcast(128), bg3, writes=[bg3])
                      ffn_macros = [list(range(i, min(i + 4, NOWN))) for i in range(1, NOWN, 4)]
                      w1c = [0]
                      w2c = [0]
                      cbc = [0]

                      def do_macro_E(mi, tiles):
                          N = 128 * len(tiles)
                          t0 = 128 * tiles[0]
                          for j in range(NFF):
                              ws = w1c[0] % NW1; w1c[0] += 1
                              S.dma("pool", W1s[ws][:, :, 0:128], w_f1[:, 128 * j:128 * j + 128].rearrange("(c p) n -> p c n", p=128),
                                    bW1[ws], writes=[bW1[ws]])
                              S.dma("pool", W1s[ws][:, :, 128:256],
                                    w_f1[:, DFF + 128 * j:DFF + 128 * j + 128].rearrange("(c p) n -> p c n", p=128),
                                    bW1[ws], writes=[bW1[ws]])
                              pa = 2 * (j % 2); pbb = pa + 1
                              if mi == 0:
                                  for kc in range(KC):
                                      S.op("pe", lambda e, ws=ws, kc=kc: e.matmul(pbank[4][:, 0:2], lhsT=W1s[ws][:, kc, 0:128],
                                                                                  rhs=h2T[:, kc, 126:128], start=(kc == 0), stop=(kc == KC - 1)),
                                           reads=[bW1[ws], bh2T], writes=[bpb[4]], inc=(kc == KC - 1))
                                  S.op("dve", lambda e, j=j: e.tensor_scalar_mul(out=carry[:, j, :], in0=pbank[4][:, 0:2], scalar1=flagt[:, 0:1]),
                                       reads=[bpb[4], bflag], writes=[bcar])
                              for kc in range(KC):
                                  S.op("pe", lambda e, ws=ws, kc=kc, pa=pa: e.matmul(pbank[pa][:, 0:N], lhsT=W1s[ws][:, kc, 0:128],
                                                                                     rhs=h2T[:, kc, t0:t0 + N], start=(kc == 0), stop=(kc == KC - 1)),
                                       reads=[bW1[ws], bh2T], writes=[bpb[pa]], inc=(kc == KC - 1))
                              for kc in range(KC):
                                  S.op("pe", lambda e, ws=ws, kc=kc, pbb=pbb: e.matmul(pbank[pbb][:, 0:N], lhsT=W1s[ws][:, kc, 128:256],
                                                                                       rhs=h2T[:, kc, t0:t0 + N], start=(kc == 0), stop=(kc == KC - 1)),
                                       reads=[bW1[ws], bh2T], writes=[bpb[pbb]], inc=(kc == KC - 1))
                              ci = cbc[0] % 2; cbc[0] += 1
                              cbuf = cb_[ci]; bc_ = bcb[ci]; ge = gel[ci]; bge = bgel[ci]
                              S.op("dve", lambda e, cbuf=cbuf, pa=pa, j=j: e.tensor_scalar(out=cbuf[:, 0:N], in0=pbank[pa][:, 0:N], scalar1=cvp[:, 2, j:j + 1],
                                                                                     scalar2=cvp[:, 3, j:j + 1], op0=ALU.mult, op1=ALU.add),
                                   reads=[bpb[pa], bcvp], writes=[bc_])
                              S.op("dve", lambda e, cbuf=cbuf, pa=pa, j=j: e.scalar_tensor_tensor(
                                  out=cbuf[:, 1:N], in0=pbank[pa][:, 0:N - 1], scalar=cvp[:, 1, j:j + 1], in1=cbuf[:, 1:N],
                                  op0=ALU.mult, op1=ALU.add), reads=[bpb[pa], bcvp, bc_], writes=[bc_])
                              S.op("dve", lambda e, cbuf=cbuf, pa=pa, j=j: e.scalar_tensor_tensor(
                                  out=cbuf[:, 2:N], in0=pbank[pa][:, 0:N - 2], scalar=cvp[:, 0, j:j + 1], in1=cbuf[:, 2:N],
                                  op0=ALU.mult, op1=ALU.add), reads=[bpb[pa], bcvp, bc_], writes=[bc_])
                              S.op("dve", lambda e, cbuf=cbuf, j=j: e.scalar_tensor_tensor(
                                  out=cbuf[:, 0:2], in0=carry[:, j, :], scalar=cvp[:, 0, j:j + 1], in1=cbuf[:, 0:2],
                                  op0=ALU.mult, op1=ALU.add), reads=[bcar, bcvp, bc_], writes=[bc_])
                              S.op("dve", lambda e, cbuf=cbuf, j=j: e.scalar_tensor_tensor(
                                  out=cbuf[:, 0:1], in0=carry[:, j, 1:2], scalar=cvp[:, 1, j:j + 1], in1=cbuf[:, 0:1],
                                  op0=ALU.mult, op1=ALU.add), reads=[bcar, bcvp, bc_], writes=[bc_])
                              S.op("dve", lambda e, pa=pa, j=j: e.tensor_copy(out=carry[:, j, :], in_=pbank[pa][:, N - 2:N]),
                                   reads=[bpb[pa], bc_], writes=[bcar])
                              S.op("act", lambda e, cbuf=cbuf, ge=ge: e.activation(out=ge[:, 0:N], in_=cbuf[:, 0:N], func=AF.Gelu_apprx_tanh),
                                   reads=[bc_], writes=[bge])
                              S.op("dve", lambda e, ge=ge, pbb=pbb, j=j: e.tensor_tensor(out=gT[:, j, 0:N], in0=pbank[pbb][:, 0:N], in1=ge[:, 0:N],
                                                                                        op=ALU.mult),
                                   reads=[bpb[pbb], bge], writes=[bgT])
                          if ELEVEL < 2:
                              return
                          S.op("pool", lambda e: e.memset(s8[:], 0.0), writes=[bs8])
                          for half in range(2):
                              for j in range(NFF):
                                  ws = w2c[0] % NW2; w2c[0] += 1
                                  S.dma("pool", W2s[ws][:], w_f2[128 * j:128 * j + 128, 1024 * half:1024 * half + 1024], bW2[ws], writes=[bW2[ws]])
                                  for ti in range(len(tiles)):
                                      for cb in range(2):
                                          S.op("pe", lambda e, ws=ws, ti=ti, cb=cb, j=j: e.matmul(
                                              pbank[2 * ti + cb][:, :], lhsT=gT[:, j, 128 * ti:128 * ti + 128], rhs=W2s[ws][:, 512 * cb:512 * cb + 512],
                                              start=(j == 0), stop=(j == NFF - 1)),
                                              reads=[bgT, bW2[ws]], writes=[bpb[2 * ti + cb]],
                                              inc=(ti == len(tiles) - 1 and cb == 1))
                              if ELEVEL < 2.5:
                                  continue
                              for ti, li in enumerate(tiles):
                                  for cb in range(2):
                                      fcol = 1024 * half + 512 * cb
                                      if cb == 0:
                                          S.op("dve", lambda e, ti=ti, cb=cb, fcol=fcol: e.tensor_copy(out=fsave[:, ti, fcol:fcol + 512],
                                                                                                     in_=pbank[2 * ti + cb][:, :]),
                                               reads=[bpb[2 * ti + cb]], writes=[bfs])
                                      else:
                                          S.op("act", lambda e, ti=ti, cb=cb, fcol=fcol: e.activation(out=fsave[:, ti, fcol:fcol + 512],
                                                                                                    in_=pbank[2 * ti + cb][:, :], func=AF.Copy),
                                               reads=[bpb[2 * ti + cb]], writes=[bfs])
                                      S.op("act", lambda e, ti=ti, cb=cb, half=half, fcol=fcol: e.activation(
                                          out=cb_[0][:], in_=fsave[:, ti, fcol:fcol + 512], func=AF.Square,
                                          accum_out=s8[:, 8 * ti + 2 * half + cb:8 * ti + 2 * half + cb + 1]),
                                          reads=[bfs], writes=[bcb[0], bs8])
                          if ELEVEL < 3:
                              return
                          for ti, li in enumerate(tiles):
                              xi = li % 2
                              S.dma("sp", x1t[xi][:], X1[128 * li:128 * li + 128, :], bx1t[xi], reads=[bX1], writes=[bx1t[xi]])
                              S.op("dve", lambda e, ti=ti: e.reduce_sum(out=s8[:, 8 * ti + 4:8 * ti + 5], in_=s8[:, 8 * ti:8 * ti + 4], axis=mybir.AxisListType.X),
                                   reads=[bs8], writes=[bs8])
                              S.op("act", lambda e, ti=ti: e.activation(out=s8[:, 8 * ti + 5:8 * ti + 6], in_=s8[:, 8 * ti + 4:8 * ti + 5], func=AF.Ln, scale=1.0 / D,
                                                                        bias=epsc[:, 0:1]), reads=[bs8, bepsc], writes=[bs8])
                              S.op("act", lambda e, ti=ti: e.activation(out=s8[:, 8 * ti + 5:8 * ti + 6], in_=s8[:, 8 * ti + 5:8 * ti + 6], func=AF.Exp, scale=-0.5),
                                   reads=[bs8], writes=[bs8])
                              for cb in range(4):
                                  S.op("dve", lambda e, ti=ti, cb=cb: e.scalar_tensor_tensor(
                                      out=fsave[:, ti, 512 * cb:512 * cb + 512], in0=fsave[:, ti, 512 * cb:512 * cb + 512],
                                      scalar=s8[:, 8 * ti + 5:8 * ti + 6], in1=g3bc[:, 512 * cb:512 * cb + 512],
                                      op0=ALU.mult, op1=ALU.mult),
                                      reads=[bfs, bs8, bg3], writes=[bfs])
                              S.op("dve", lambda e, ti=ti, xi=xi: e.tensor_tensor(out=x1t[xi][:], in0=x1t[xi][:], in1=fsave[:, ti, :], op=ALU.add),
                                   reads=[bfs, bx1t[xi]], writes=[bx1t[xi]])
                              S.dma("sp", y_out[128 * (li - 1):128 * li, :], x1t[xi][:], bx1t[xi], reads=[bx1t[xi]])
                      for mi_, tiles_ in enumerate(ffn_macros):
                          do_macro_E(mi_, tiles_)
                      if debug:
                          dF = dout("dbg_F", [128, 4, 2048], F32)
                          dG = dout("dbg_G", [128, NFF, 512], BF16)
                          bd = S.buf("dbgE")
                          S.barrier()
                          S.dma("sp", dF, fsave[:], bd, reads=[bfs])
                          S.dma("sp", dG, gT[:], bd, reads=[bgT])
                      S.barrier()
                      print("stage E sbuf remaining", nc.sbuf_bytes_remaining, "insts", S.ninst)
                      S.emit()

        S.barrier()
        S.emit()
    return nc, dbg


def _host_consts(nb, c):
    NT = NCORES * nb
    NG = 1 + nb // 2
    p = np.arange(128)
    def gt(li):
        T = nb * c + li - 1
        return 0 if T < 0 else T
    groups = [[0]] + [[2 * g - 1, 2 * g] for g in range(1, NG)]
    posrel = np.zeros((128, NG, NT), np.float32)
    for g, tl in enumerate(groups):
        Tl = [gt(li) for li in tl]
        qref = 128 * Tl[-1] + 127
        for kt in range(NT):
            if kt <= Tl[-1]:
                posrel[:, g, kt] = 128 * kt + p - qref
            else:
                posrel[:, g, kt] = NEG
    maskp = np.zeros((128, 16, 256), np.float32)
    q = np.arange(256)
    for cp in range(8):
        for ab in range(2):
            if cp < c:
                m = np.ones((128, 256), np.float32)
            elif cp > c:
                m = np.zeros((128, 256), np.float32)
            else:
                m = ((128 * ab + p)[:, None] <= q[None, :]).astype(np.float32)
            maskp[:, cp * 2 + ab, :] = m
    maskh = np.zeros((128, 8, 128), np.float32)
    qh = np.arange(128)
    Th = gt(0)
    for slot in range(8):
        kt = 0 if slot == 0 else nb * slot - 1
        kpos = 128 * kt + p
        qpos = 128 * Th + qh
        maskh[:, slot, :] = (kpos[:, None] <= qpos[None, :]).astype(np.float32)
    sel = np.zeros((128, 8), np.float32)
    sel[:, c] = 1.0
    flag = np.full((128, 1), 0.0 if c == 0 else 1.0, np.float32)
    return dict(posrel=posrel.reshape(128, NG * NT), maskp=maskp.astype(bf16_np), maskh=maskh.astype(bf16_np),
                sel=sel, flag=flag)


def make_in_maps(inputs, nb):
    x = np.asarray(inputs["x"], np.float32)[0]
    S_ALL = NCORES * nb * 128
    assert x.shape[0] == S_ALL
    w_in = np.ascontiguousarray(np.asarray(inputs["w_in"], np.float32)[0])
    walpha = np.concatenate([np.asarray(inputs["w_alpha_up"], np.float32)[0],
                             np.asarray(inputs["b_alpha"], np.float32)[0][None, :]], axis=0)
    nrm = np.stack([np.asarray(inputs[k], np.float32)[0] for k in
                    ("attn_pre_norm", "attn_post_norm", "ffn_pre_norm", "ffn_post_norm")])
    hnrm = np.stack([np.asarray(inputs["gla_norm"], np.float32)[0], np.asarray(inputs["diff_norm"], np.float32)[0]])
    lamv = np.stack([np.asarray(inputs[k], np.float32)[0] for k in ("lambda_q1", "lambda_k1", "lambda_q2", "lambda_k2")])
    cw = np.asarray(inputs["conv_w"], np.float32)[0]
    cb = np.asarray(inputs["conv_b"], np.float32)[0]
    convp = np.stack([cw[0], cw[1], cw[2], cb]).reshape(4, NFF, 128).transpose(2, 0, 1).copy()
    ident = np.eye(128, dtype=np.float32).astype(bf16_np)
    j = np.arange(128)[:, None]
    i = np.arange(128)[None, :]
    tri = np.stack([(j > i) * (-1.0 / 16), (j <= i) * (-1.0 / 16), (j <= i) * 1.0, np.zeros((128, 128))]).astype(np.float32)
    common = dict(x_all=x, w_in=w_in, walpha=walpha, w_o=np.asarray(inputs["w_o"], np.float32)[0],
                  w_f1=np.asarray(inputs["w_ffn_in"], np.float32)[0], w_f2=np.asarray(inputs["w_ffn_out"], np.float32)[0],
                  nrm=nrm, hnrm=hnrm, lamv=lamv, convp=convp, ident=ident, tri=tri)
    maps = []
    for c in range(NCORES):
        lo = (nb * c - 1) * 128
        if c == 0:
            xo = np.concatenate([x[0:128], x[0:nb * 128]], axis=0)
        else:
            xo = x[lo:lo + (nb + 1) * 128]
        m = dict(common)
        m["x_own"] = np.ascontiguousarray(xo)
        m.update(_host_consts(nb, c))
        maps.append(m)
    return maps


_CACHE = {}


def kernel(**inputs):
    nb = 16
    if nb not in _CACHE:
        _CACHE[nb] = build(nb)[0]
    nc = _CACHE[nb]
    maps = make_in_maps(inputs, nb)
    res = run_bass_kernel_spmd(nc, maps, core_ids=list(range(NCORES)))
    out = np.concatenate([np.asarray(res.results[c]["y"]) for c in range(NCORES)], axis=0)
    return out.reshape(1, NCORES * nb * 128, D).astype(np.float32)
```

```python
import numpy as np
import ml_dtypes
from contextlib import ExitStack
import concourse.bass as bass
import concourse.mybir as mybir
from concourse.bass_utils import run_bass_kernel_spmd

F32 = mybir.dt.float32
BF16 = mybir.dt.bfloat16
AF = mybir.ActivationFunctionType
ALU = mybir.AluOpType
bf16_np = ml_dtypes.bfloat16

NCORES = 8
D = 2048
KC = 16
DFF = 5632
NFF = 44
EPS = 1e-6
C_GQ, C_GK, C_GV, C_GG, C_GA, C_DQ, C_DK, C_DV = 0, 512, 1024, 2048, 3072, 3088, 4112, 5136
SLOPES = [2.0 ** (-8.0 * (h + 1) / 4) for h in range(4)]
LAMBDA_INIT = 0.8 - 0.6 * 1.0
NEG = -1.0e6

ENGS = ("pe", "act", "dve", "pool", "sp")


class Buf:
    __slots__ = ("name", "wt", "rts", "sem", "semcnt")

    def __init__(self, name):
        self.name = name
        self.wt = {}
        self.rts = {}
        self.sem = None
        self.semcnt = 0


class Sched:
    def __init__(self, nc, stack):
        self.nc = nc
        self.stack = stack
        self.streams = {e: [] for e in ENGS}
        self.esem = {e: stack.enter_context(nc.semaphore("s_" + e)) for e in ENGS}
        self.cnt = {e: 0 for e in ENGS}
        self.seen = {e: {} for e in ENGS}
        self.semh = dict(self.esem)
        self.nsem = 0
        self.allbufs = []
        self.ninst = 0

    def buf(self, name):
        b = Buf(name)
        self.allbufs.append(b)
        return b

    def _bufsem(self, b):
        if b.sem is None:
            self.nsem += 1
            key = "d%d" % self.nsem
            h = self.stack.enter_context(self.nc.semaphore(key))
            b.sem = key
            self.semh[key] = h
        return b.sem

    def _need(self, eng, tickets):
        best = {}
        for t in tickets:
            if t is None:
                continue
            k, v = t
            if v > best.get(k, 0):
                best[k] = v
        for k, v in best.items():
            if self.seen[eng].get(k, 0) >= v:
                continue
            self.seen[eng][k] = v
            h = self.semh[k]
            self.ninst += 1
            self.streams[eng].append(lambda e, h=h, v=v: e.wait_ge(h, v))

    def _deps(self, eng, reads, writes):
        ts = []
        for b in reads:
            ts.extend(b.wt.items())
        for b in writes:
            ts.extend(b.wt.items())
            ts.extend(b.rts.items())
        out = []
        for t in ts:
            if t is None:
                continue
            if t[0] == eng and eng == "pe":
                continue
            out.append(t)
        return out

    def op(self, eng, fn, reads=(), writes=(), inc=True):
        self._need(eng, self._deps(eng, reads, writes))
        self.ninst += 1
        if inc:
            self.cnt[eng] += 1
            v = self.cnt[eng]
            h = self.esem[eng]
            self.streams[eng].append(lambda e, fn=fn, h=h: fn(e).then_inc(h, 1))
        else:
            v = self.cnt[eng] + 1
            self.streams[eng].append(lambda e, fn=fn: fn(e))
        t = (eng, v)
        for b in reads:
            if b.rts.get(t[0], 0) < t[1]:
                b.rts[t[0]] = t[1]
        for b in writes:
            if b.wt.get(t[0], 0) < t[1]:
                b.wt[t[0]] = t[1]
            b.rts = {}
        return t

    def dma(self, q, out, in_, track, reads=(), writes=()):
        self._need(q, self._deps(q, reads, writes))
        key = self._bufsem(track)
        track.semcnt += 16
        v = track.semcnt
        h = self.semh[key]
        self.ninst += 1
        self.streams[q].append(lambda e, out=out, in_=in_, h=h: e.dma_start(out=out, in_=in_).then_inc(h, 16))
        t = (key, v)
        for b in reads:
            if b.rts.get(t[0], 0) < t[1]:
                b.rts[t[0]] = t[1]
        for b in writes:
            if b.wt.get(t[0], 0) < t[1]:
                b.wt[t[0]] = t[1]
            b.rts = {}
        return t

    def wait(self, eng, tickets):
        self._need(eng, [t for t in tickets if t is not None])

    def barrier(self):
        ts = [(e, self.cnt[e]) for e in ENGS if self.cnt[e] > 0]
        for b in self.allbufs:
            if b.sem is not None and b.semcnt > 0:
                ts.append((b.sem, b.semcnt))
        for e in ENGS:
            self._need(e, [t for t in ts if t[0] != e])

    def emit(self):
        nc = self.nc
        streams = self.streams
        self.streams = {e: [] for e in ENGS}
        with nc.Block() as block:
            @block.tensor
            def _(e):
                for f in streams["pe"]:
                    f(e)

            @block.scalar
            def _(e):
                for f in streams["act"]:
                    f(e)

            @block.vector
            def _(e):
                for f in streams["dve"]:
                    f(e)

            @block.gpsimd
            def _(e):
                for f in streams["pool"]:
                    f(e)

            @block.sync
            def _(e):
                for f in streams["sp"]:
                    f(e)


def build(nb=16, stages="ABCDE", debug=False):
    NT = NCORES * nb
    S_ALL = NT * 128
    NOWN = nb + 1
    S_OWN = NOWN * 128
    NG = 1 + nb // 2
    import os
    ELEVEL = float(os.environ.get("ELEVEL", "9"))
    nc = bass.Bass("TRN2", target_bir_lowering=False)

    def din(name, shape, dt=F32):
        return nc.dram_tensor(name, list(shape), dt, kind="ExternalInput").ap()

    x_all = din("x_all", [S_ALL, D])
    x_own = din("x_own", [S_OWN, D])
    w_in = din("w_in", [D, 6160])
    walpha = din("walpha", [17, 512])
    w_o = din("w_o", [D, D])
    w_f1 = din("w_f1", [D, 2 * DFF])
    w_f2 = din("w_f2", [DFF, D])
    nrm = din("nrm", [4, D])
    hnrm = din("hnrm", [2, 256])
    lamv = din("lamv", [4, 128])
    convp = din("convp", [128, 4, NFF])
    ident_d = din("ident", [128, 128], BF16)
    tri_d = din("tri", [4, 128, 128])
    sel_d = din("sel", [128, 8])
    flag_d = din("flag", [128, 1])
    posrel_d = din("posrel", [128, NG * NT])
    maskp_d = din("maskp", [128, 16, 256], BF16)
    maskh_d = din("maskh", [128, 8, 128], BF16)
    y_out = nc.dram_tensor("y", [nb * 128, D], F32, kind="ExternalOutput").ap()
    dbg = {}

    def dout(name, shape, dt=F32):
        ap = nc.dram_tensor(name, list(shape), dt, kind="ExternalOutput").ap()
        dbg[name] = ap
        return ap

    KT = nc.dram_tensor("KT", [8, 128, S_ALL], BF16, kind="Internal").ap()
    VV = nc.dram_tensor("VV", [S_ALL, 1024], BF16, kind="Internal").ap()
    QT = nc.dram_tensor("QT", [8, 128, S_OWN], BF16, kind="Internal").ap()
    OM = nc.dram_tensor("OM", [S_OWN, D], BF16, kind="Internal").ap()
    X1 = nc.dram_tensor("X1", [S_OWN, D], F32, kind="Internal").ap()
    SOWN = nc.dram_tensor("SOWN", [128, 1024], F32, kind="Internal").ap()

    with ExitStack() as top:
        S = Sched(nc, top)
        bKT, bVV, bQT, bOM, bX1, bSOWN = (S.buf(n) for n in ("KT", "VV", "QT", "OM", "X1", "SOWN"))
        pbank = [top.enter_context(nc.psum_tensor("pb%d" % i, [128, 512], F32)) for i in range(8)]

        def w_view(w_ap, c0, ncol):
            return w_ap[:, c0:c0 + ncol].rearrange("(c p) n -> p c n", p=128)

        def rstd_ops(t, bt, n):
            S.op("act", lambda e: e.activation(out=t[:, 1:2], in_=t[:, 0:1], func=AF.Ln, scale=1.0 / n, bias=epsc[:, 0:1]),
                 reads=[bt, bepsc], writes=[bt])
            S.op("act", lambda e: e.activation(out=t[:, 1:2], in_=t[:, 1:2], func=AF.Exp, scale=-0.5),
                 reads=[bt], writes=[bt])

        epsc = top.enter_context(nc.sbuf_tensor("epsc", [128, 2], F32)); bepsc = S.buf("epsc")
        S.op("pool", lambda e: e.memset(epsc[:, 0:1], EPS), writes=[bepsc])
        S.op("pool", lambda e: e.memset(epsc[:, 1:2], 1.0), writes=[bepsc])

        if "A" in stages:
            with ExitStack() as st:
                sb = lambda n, s, d: st.enter_context(nc.sbuf_tensor("A_" + n, list(s), d))
                wk = sb("wk", [128, KC, 1024], BF16); bwk = S.buf("wk")
                wv = sb("wv", [128, KC, 1024], BF16); bwv = S.buf("wv")
                wgk = sb("wgk", [128, KC, 512], BF16); bwgk = S.buf("wgk")
                wgv = sb("wgv", [128, KC, 1024], BF16); bwgv = S.buf("wgv")
                wga = sb("wga", [128, KC, 16], BF16); bwga = S.buf("wga")
                gbc = sb("gbc", [128, D], F32); bgbc = S.buf("gbc")
                wal = sb("wal", [17, 512], F32); bwal = S.buf("wal")
                ident = sb("ident", [128, 128], BF16); bid = S.buf("ident")
                tri = sb("tri", [128, 4, 128], F32); btri = S.buf("tri")
                negcol = sb("negcol", [128, 1], F32); bneg = S.buf("negcol")
                selt = sb("selt", [128, 8], F32); bsel = S.buf("selt")
                xs = [sb("xs%d" % i, [128, D], F32) for i in range(2)]; bxs = [S.buf("xs%d" % i) for i in range(2)]
                ssq = [sb("ssq%d" % i, [128, 2], F32) for i in range(2)]; bssq = [S.buf("ssq%d" % i) for i in range(2)]
                hb = [sb("hb%d" % i, [128, D], BF16) for i in range(2)]; bhb = [S.buf("hb%d" % i) for i in range(2)]
                hT = [sb("hT%d" % i, [128, KC, 512], BF16) for i in range(2)]; bhT = [S.buf("hT%d" % i) for i in range(2)]
                kst = [sb("kst%d" % i, [128, 512], BF16) for i in range(2)]; bkst = [S.buf("kst%d" % i) for i in range(2)]
                vst = [sb("vst%d" % i, [128, 1024], BF16) for i in range(2)]; bvst = [S.buf("vst%d" % i) for i in range(2)]
                gaT = sb("gaT", [17, 512], F32); bgaT = S.buf("gaT")
                e1 = sb("e1", [128, 512], F32); be1 = S.buf("e1")
                spt = sb("spt", [128, 512], F32); bspt = S.buf("spt")
                E3 = sb("E3", [128, 512], F32); bE3 = S.buf("E3")
                dec = sb("dec", [128, 4], F32); bdec = S.buf("dec")
                khat = sb("khat", [128, 512], BF16); bkhat = S.buf("khat")
                gvb = sb("gvb", [128, 1024], BF16); bgvb = S.buf("gvb")
                Sst = sb("Sst", [128, 1024], F32); bS = S.buf("Sst")
                Sown = sb("Sown", [128, 1024], F32); bSo = S.buf("Sown")
                bpb = [S.buf("pbA%d" % i) for i in range(8)]

                S.dma("sp", ident[:], ident_d, bid, writes=[bid])
                S.dma("sp", tri[:], tri_d.rearrange("a p n -> p a n"), btri, writes=[btri])
                S.dma("sp", selt[:], sel_d, bsel, writes=[bsel])
                S.dma("sp", wal[:], walpha, bwal, writes=[bwal])
                S.dma("sp", gbc[:], nrm[0:1, :].partition_broadcast(128), bgbc, writes=[bgbc])
                S.op("pool", lambda e: e.memset(negcol[:], -1.0 / 16.0), writes=[bneg])
                S.op("pool", lambda e: e.memset(gaT[:], 1.0), writes=[bgaT])
                S.op("pool", lambda e: e.memset(Sst[:], 0.0), writes=[bS])
                S.op("pool", lambda e: e.memset(Sown[:], 0.0), writes=[bSo])
                for kc in range(KC):
                    S.dma("pool", wk[:, kc, :], w_view(w_in, C_DK, 1024)[:, kc, :], bwk, writes=[bwk])
                for kc in range(KC):
                    S.dma("pool", wv[:, kc, :], w_view(w_in, C_DV, 1024)[:, kc, :], bwv, writes=[bwv])
                for kc in range(KC):
                    S.dma("pool", wgk[:, kc, :], w_view(w_in, C_GK, 512)[:, kc, :], bwgk, writes=[bwgk])
                for kc in range(KC):
                    S.dma("pool", wgv[:, kc, :], w_view(w_in, C_GV, 1024)[:, kc, :], bwgv, writes=[bwgv])
                S.dma("pool", wga[:], w_view(w_in, C_GA, 16), bwga, writes=[bwga])

                ptr_bf = [pbank[0][:].bitcast(BF16), pbank[1][:].bitcast(BF16)]
                cp_tgl = [0]

                def evac(out, in_, reads, writes, eng=None):
                    if eng is None:
                        eng = ("act", "dve")[cp_tgl[0] % 2]
                        cp_tgl[0] += 1
                    if eng == "act":
                        return S.op("act", lambda e: e.activation(out=out, in_=in_, func=AF.Copy), reads=reads, writes=writes)
                    return S.op("dve", lambda e: e.tensor_copy(out=out, in_=in_), reads=reads, writes=writes)

                snap_tiles = {nb * cc - 2: cc for cc in range(1, 8)}
                for M in range(NT // 4):
                    ms = M % 2
                    for j in range(4):
                        t = 4 * M + j
                        sl = t % 2
                        S.dma("sp", xs[sl][:], x_all[128 * t:128 * t + 128, :], bxs[sl], writes=[bxs[sl]])
                        S.op("pool", lambda e, sl=sl: e.memset(ssq[sl][:], 0.0), writes=[bssq[sl]])
                        S.op("act", lambda e, sl=sl: e.activation(out=hb[sl][:], in_=xs[sl][:], func=AF.Square,
                                                                 accum_out=ssq[sl][:, 0:1]),
                             reads=[bxs[sl]], writes=[bhb[sl], bssq[sl]])
                        rstd_ops(ssq[sl], bssq[sl], D)
                        S.op("dve", lambda e, sl=sl: e.scalar_tensor_tensor(out=hb[sl][:], in0=xs[sl][:], scalar=ssq[sl][:, 1:2],
                                                                           in1=gbc[:], op0=ALU.mult, op1=ALU.mult),
                             reads=[bxs[sl], bssq[sl], bgbc], writes=[bhb[sl]])
                        for half in range(2):
                            pt = ptr_bf[half]
                            for k8 in range(8):
                                kc = half * 8 + k8
                                S.op("pe", lambda e, pt=pt, k8=k8, kc=kc, sl=sl: e.transpose(
                                    out=pt[:, 128 * k8:128 * k8 + 128], in_=hb[sl][:, 128 * kc:128 * kc + 128], identity=ident[:]),
                                    reads=[bhb[sl], bid], writes=[bpb[half]], inc=(k8 == 7))
                            evac(hT[ms][:, half * 8:half * 8 + 8, 128 * j:128 * j + 128],
                                 pt.rearrange("p (k n) -> p k n", k=8), [bpb[half]], [bhT[ms]])
                    for cc in range(8):
                        pk = pbank[2 + cc % 2]; bpk = bpb[2 + cc % 2]
                        for kc in range(KC):
                            S.op("pe", lambda e, pk=pk, cc=cc, kc=kc, ms=ms: e.matmul(
                                pk[:, :], lhsT=wk[:, kc, 128 * cc:128 * cc + 128], rhs=hT[ms][:, kc, :],
                                start=(kc == 0), stop=(kc == KC - 1)),
                                reads=[bwk, bhT[ms]], writes=[bpk], inc=(kc == KC - 1))
                        ks = cc % 2
                        evac(kst[ks][:], pk[:, :], [bpk], [bkst[ks]])
                        S.dma("sp", KT[cc, :, 512 * M:512 * M + 512], kst[ks][:], bkst[ks], reads=[bkst[ks]], writes=[bKT])
                    for j in range(4):
                        t = 4 * M + j
                        vs = j % 2
                        for cb in range(2):
                            pv = pbank[4 + cb]; bpv = bpb[4 + cb]
                            for kc in range(KC):
                                S.op("pe", lambda e, pv=pv, cb=cb, kc=kc, ms=ms, j=j: e.matmul(
                                    pv[:, :], lhsT=hT[ms][:, kc, 128 * j:128 * j + 128], rhs=wv[:, kc, 512 * cb:512 * cb + 512],
                                    start=(kc == 0), stop=(kc == KC - 1)),
                                    reads=[bwv, bhT[ms]], writes=[bpv], inc=(kc == KC - 1))
                            evac(vst[vs][:, 512 * cb:512 * cb + 512], pv[:, :], [bpv], [bvst[vs]])
                        S.dma("sp", VV[128 * t:128 * t + 128, :], vst[vs][:], bvst[vs], reads=[bvst[vs]], writes=[bVV])
                    pg = pbank[6]; bpg = bpb[6]
                    for kc in range(KC):
                        S.op("pe", lambda e, kc=kc, ms=ms: e.matmul(
                            pg[0:16, :], lhsT=wga[:, kc, :], rhs=hT[ms][:, kc, :], start=(kc == 0), stop=(kc == KC - 1)),
                            reads=[bwga, bhT[ms]], writes=[bpg], inc=(kc == KC - 1))
                    evac(gaT[0:16, :], pg[0:16, :], [bpg], [bgaT], eng="dve")
                    for j in range(4):
                        t = 4 * M + j
                        pz = pbank[7]; bpz = bpb[7]
                        S.op("pe", lambda e, j=j: e.matmul(pz[:, :], lhsT=gaT[:, 128 * j:128 * j + 128], rhs=wal[:, :],
                                                           start=True, stop=True),
                             reads=[bgaT, bwal], writes=[bpz])
                        S.op("act", lambda e: e.activation(out=e1[:], in_=pz[:, :], func=AF.Exp, scale=-1.0),
                             reads=[bpz], writes=[be1])
                        S.op("act", lambda e: e.activation(out=spt[:], in_=e1[:], func=AF.Ln, bias=epsc[:, 1:2]),
                             reads=[be1, bepsc], writes=[bspt])
                        S.op("pe", lambda e: e.matmul(pz[:, :], lhsT=tri[:, 0, :], rhs=spt[:], start=True, stop=True),
                             reads=[btri, bspt], writes=[bpz])
                        S.op("act", lambda e: e.activation(out=E3[:], in_=pz[:, :], func=AF.Exp),
                             reads=[bpz], writes=[bE3])
                        for h in range(4):
                            S.op("pe", lambda e, h=h: e.matmul(pg[:, 4 * h:4 * h + 1], lhsT=spt[:, 128 * h:128 * h + 128],
                                                               rhs=negcol[:], start=True, stop=True),
                                 reads=[bspt, bneg], writes=[bpg], inc=(h == 3))
                        S.op("act", lambda e: e.activation(out=dec[:], in_=pg[:, 0:16].rearrange("p (h f) -> p h f", f=4)[:, :, 0],
                                                           func=AF.Exp),
                             reads=[bpg], writes=[bdec])
                        pk = pbank[2]; bpk = bpb[2]
                        for kc in range(KC):
                            S.op("pe", lambda e, kc=kc, ms=ms, j=j: e.matmul(
                                pk[:, :], lhsT=hT[ms][:, kc, 128 * j:128 * j + 128], rhs=wgk[:, kc, :],
                                start=(kc == 0), stop=(kc == KC - 1)),
                                reads=[bwgk, bhT[ms]], writes=[bpk], inc=(kc == KC - 1))
                        S.op("dve", lambda e: e.tensor_tensor(out=khat[:], in0=pk[:, :], in1=E3[:], op=ALU.mult),
                             reads=[bpk, bE3], writes=[bkhat])
                        for cb in range(2):
                            pv = pbank[4 + cb]; bpv = bpb[4 + cb]
                            for kc in range(KC):
                                S.op("pe", lambda e, pv=pv, cb=cb, kc=kc, ms=ms, j=j: e.matmul(
                                    pv[:, :], lhsT=hT[ms][:, kc, 128 * j:128 * j + 128], rhs=wgv[:, kc, 512 * cb:512 * cb + 512],
                                    start=(kc == 0), stop=(kc == KC - 1)),
                                    reads=[bwgv, bhT[ms]], writes=[bpv], inc=(kc == KC - 1))
                            evac(gvb[:, 512 * cb:512 * cb + 512], pv[:, :], [bpv], [bgvb])
                        for h in range(4):
                            pu = pbank[3]; bpu = bpb[3]
                            hh = h % 2
                            S.op("pe", lambda e, h=h, hh=hh: e.matmul(pu[:, 256 * hh:256 * hh + 256], lhsT=khat[:, 128 * h:128 * h + 128],
                                                                      rhs=gvb[:, 256 * h:256 * h + 256], start=True, stop=True),
                                 reads=[bkhat, bgvb], writes=[bpu])
                            S.op("dve", lambda e, h=h, hh=hh: e.scalar_tensor_tensor(
                                out=Sst[:, 256 * h:256 * h + 256], in0=Sst[:, 256 * h:256 * h + 256], scalar=dec[:, h:h + 1],
                                in1=pu[:, 256 * hh:256 * hh + 256], op0=ALU.mult, op1=ALU.add),
                                reads=[bS, bdec, bpu], writes=[bS])
                        if t in snap_tiles:
                            cc = snap_tiles[t]
                            S.op("dve", lambda e, cc=cc: e.scalar_tensor_tensor(
                                out=Sown[:], in0=Sst[:], scalar=selt[:, cc:cc + 1], in1=Sown[:], op0=ALU.mult, op1=ALU.add),
                                reads=[bS, bsel, bSo], writes=[bSo])
                S.dma("sp", SOWN, Sown[:], bSo, reads=[bSo], writes=[bSOWN])
                if debug:
                    d1 = dout("dbg_KT", [8, 128, S_ALL], BF16)
                    d2 = dout("dbg_VV", [S_ALL, 1024], BF16)
                    d3 = dout("dbg_SOWN", [128, 1024])
                    bd = S.buf("dbgA")
                    S.barrier()
                    S.dma("sp", d1, KT, bd, reads=[bKT])
                    S.dma("sp", d2, VV, bd, reads=[bVV])
                    S.dma("sp", d3, SOWN, bd, reads=[bSOWN])
                S.barrier()
                print("stage A sbuf remaining", nc.sbuf_bytes_remaining, "insts", S.ninst)
                S.emit()

        own_macros = [list(range(i, min(i + 2, NOWN))) for i in range(0, NOWN, 2)]
        if "B" in stages:
            with ExitStack() as st:
                sb = lambda n, s, d: st.enter_context(nc.sbuf_tensor("B_" + n, list(s), d))
                wgq = sb("wgq", [128, KC, 512], BF16); bwgq = S.buf("wgq")
                wgk = sb("wgk", [128, KC, 512], BF16); bwgk = S.buf("wgk")
                wgv = sb("wgv", [128, KC, 1024], BF16); bwgv = S.buf("wgv")
                wgg = sb("wgg", [128, KC, 1024], BF16); bwgg = S.buf("wgg")
                wga = sb("wga", [128, KC, 16], BF16); bwga = S.buf("wga")
                wdq = sb("wdq", [128, KC, 1024], BF16); bwdq = S.buf("wdq")
                gbc = sb("gbc", [128, D], F32); bgbc = S.buf("gbc")
                wal = sb("wal", [17, 512], F32); bwal = S.buf("wal")
                ident = sb("ident", [128, 128], BF16); bid = S.buf("ident")
                tri = sb("tri", [128, 4, 128], F32); btri = S.buf("tri")
                mask4 = sb("mask4", [128, 4, 128], F32); bm4 = S.buf("mask4")
                flagt = sb("flagt", [128, 1], F32); bflag = S.buf("flagt")
                glab = sb("glab", [128, 256], F32); bglab = S.buf("glab")
                xs = sb("xs", [128, D], F32); bxs = S.buf("xs")
                ssq = sb("ssq", [128, 2], F32); bssq = S.buf("ssq")
                hb = sb("hb", [128, D], BF16); bhb = S.buf("hb")
                hT = sb("hT", [128, KC, 256], BF16); bhT = S.buf("hT")
                qTf = sb("qTf", [128, 4, 256], F32); bqTf = S.buf("qTf")
                kTf = sb("kTf", [128, 4, 256], F32); bkTf = S.buf("kTf")
                qst = [sb("qst%d" % i, [128, 256], BF16) for i in range(2)]; bqst = [S.buf("qst%d" % i) for i in range(2)]
                gaT = sb("gaT", [17, 256], F32); bgaT = S.buf("gaT")
                e1 = sb("e1", [128, 512], F32); be1 = S.buf("e1")
                spt = sb("spt", [128, 512], F32); bspt = S.buf("spt")
                E3 = sb("E3", [128, 512], F32); bE3 = S.buf("E3")
                E1T = sb("E1T", [128, 512], F32); bE1T = S.buf("E1T")
                E2T = sb("E2T", [128, 512], F32); bE2T = S.buf("E2T")
                qtl = sb("qtl", [128, 4, 128], BF16); bqtl = S.buf("qtl")
                ktl = sb("ktl", [128, 4, 128], BF16); bktl = S.buf("ktl")
                AT = sb("AT", [128, 4, 128], BF16); bAT = S.buf("AT")
                khat = sb("khat", [128, 512], BF16); bkhat = S.buf("khat")
                gvb = sb("gvb", [128, 1024], BF16); bgvb = S.buf("gvb")
                gate = sb("gate", [128, 1024], F32); bgate = S.buf("gate")
                Sst = sb("Sst", [128, 1024], F32); bS = S.buf("Sst")
                Sbb = sb("Sbb", [128, 1024], BF16); bSb = S.buf("Sbb")
                oss = sb("oss", [128, 8], F32); boss = S.buf("oss")
                omA = [sb("omA%d" % i, [128, 1024], BF16) for i in range(2)]; bomA = [S.buf("omA%d" % i) for i in range(2)]
                bpb = [S.buf("pbB%d" % i) for i in range(8)]

                S.dma("sp", ident[:], ident_d, bid, writes=[bid])
                S.dma("sp", tri[:], tri_d.rearrange("a p n -> p a n"), btri, writes=[btri])
                for h in range(4):
                    S.dma("sp", mask4[:, h, :], tri_d[2], bm4, writes=[bm4])
                S.dma("sp", flagt[:], flag_d, bflag, writes=[bflag])
                S.dma("sp", wal[:], walpha, bwal, writes=[bwal])
                S.dma("sp", gbc[:], nrm[0:1, :].partition_broadcast(128), bgbc, writes=[bgbc])
                S.dma("sp", glab[:], hnrm[0:1, :].partition_broadcast(128), bglab, writes=[bglab])
                S.dma("sp", Sst[:], SOWN, bS, reads=[bSOWN], writes=[bS])
                S.op("pool", lambda e: e.memset(gaT[:], 1.0), writes=[bgaT])
                for (wt, bw, c0, ncol) in ((wgq, bwgq, C_GQ, 512), (wgk, bwgk, C_GK, 512), (wgv, bwgv, C_GV, 1024),
                                           (wgg, bwgg, C_GG, 1024), (wdq, bwdq, C_DQ, 1024)):
                    for kc in range(KC):
                        S.dma("pool", wt[:, kc, :], w_view(w_in, c0, ncol)[:, kc, :], bw, writes=[bw])
                S.dma("pool", wga[:], w_view(w_in, C_GA, 16), bwga, writes=[bwga])
                S.op("act", lambda e: e.activation(out=Sbb[:], in_=Sst[:], func=AF.Copy), reads=[bS], writes=[bSb])

                ptr_bf = [pbank[0][:].bitcast(BF16), pbank[1][:].bitcast(BF16)]
                tg = [0]

                def evacB(out, in_, reads, writes, eng=None):
                    if eng is None:
                        eng = ("act", "dve")[tg[0] % 2]
                        tg[0] += 1
                    if eng == "act":
                        return S.op("act", lambda e: e.activation(out=out, in_=in_, func=AF.Copy), reads=reads, writes=writes)
                    return S.op("dve", lambda e: e.tensor_copy(out=out, in_=in_), reads=reads, writes=writes)

                def proj_tm(wt, bw, j, cb, pbk):
                    for kc in range(KC):
                        S.op("pe", lambda e, kc=kc: e.matmul(pbank[pbk][:, :], lhsT=hT[:, kc, 128 * j:128 * j + 128],
                                                             rhs=wt[:, kc, 512 * cb:512 * cb + 512],
                                                             start=(kc == 0), stop=(kc == KC - 1)),
                             reads=[bw, bhT], writes=[bpb[pbk]], inc=(kc == KC - 1))

                omi_c = [0]

                def do_macro_B(tiles):
                    N = 128 * len(tiles)
                    for j, li in enumerate(tiles):
                        S.dma("sp", xs[:], x_own[128 * li:128 * li + 128, :], bxs, writes=[bxs])
                        S.op("pool", lambda e: e.memset(ssq[:], 0.0), writes=[bssq])
                        S.op("act", lambda e: e.activation(out=hb[:], in_=xs[:], func=AF.Square, accum_out=ssq[:, 0:1]),
                             reads=[bxs], writes=[bhb, bssq])
                        rstd_ops(ssq, bssq, D)
                        S.op("dve", lambda e: e.scalar_tensor_tensor(out=hb[:], in0=xs[:], scalar=ssq[:, 1:2], in1=gbc[:],
                                                                    op0=ALU.mult, op1=ALU.mult),
                             reads=[bxs, bssq, bgbc], writes=[bhb])
                        for half in range(2):
                            pt = ptr_bf[half]
                            for k8 in range(8):
                                kc = half * 8 + k8
                                S.op("pe", lambda e, pt=pt, k8=k8, kc=kc: e.transpose(
                                    out=pt[:, 128 * k8:128 * k8 + 128], in_=hb[:, 128 * kc:128 * kc + 128], identity=ident[:]),
                                    reads=[bhb, bid], writes=[bpb[half]], inc=(k8 == 7))
                            evacB(hT[:, half * 8:half * 8 + 8, 128 * j:128 * j + 128],
                                  pt.rearrange("p (k n) -> p k n", k=8), [bpb[half]], [bhT])
                    for (wt, bw, dst, bdst) in ((wgq, bwgq, qTf, bqTf), (wgk, bwgk, kTf, bkTf)):
                        for cc in range(4):
                            pk = 2 + cc % 2
                            for kc in range(KC):
                                S.op("pe", lambda e, wt=wt, pk=pk, cc=cc, kc=kc: e.matmul(
                                    pbank[pk][:, 0:N], lhsT=wt[:, kc, 128 * cc:128 * cc + 128], rhs=hT[:, kc, 0:N],
                                    start=(kc == 0), stop=(kc == KC - 1)),
                                    reads=[bw, bhT], writes=[bpb[pk]], inc=(kc == KC - 1))
                            evacB(dst[:, cc, 0:N], pbank[pk][:, 0:N], [bpb[pk]], [bdst])
                    for cc in range(8):
                        pk = 2 + cc % 2
                        for kc in range(KC):
                            S.op("pe", lambda e, pk=pk, cc=cc, kc=kc: e.matmul(
                                pbank[pk][:, 0:N], lhsT=wdq[:, kc, 128 * cc:128 * cc + 128], rhs=hT[:, kc, 0:N],
                                start=(kc == 0), stop=(kc == KC - 1)),
                                reads=[bwdq, bhT], writes=[bpb[pk]], inc=(kc == KC - 1))
                        qs = cc % 2
                        S.op("act", lambda e, pk=pk, qs=qs: e.mul(out=qst[qs][:, 0:N], in_=pbank[pk][:, 0:N], mul=128.0 ** -0.5),
                             reads=[bpb[pk]], writes=[bqst[qs]])
                        S.dma("sp", QT[cc, :, 128 * tiles[0]:128 * tiles[0] + N], qst[qs][:, 0:N], bqst[qs],
                              reads=[bqst[qs]], writes=[bQT])
                    for kc in range(KC):
                        S.op("pe", lambda e, kc=kc: e.matmul(pbank[4][0:16, 0:N], lhsT=wga[:, kc, :], rhs=hT[:, kc, 0:N],
                                                             start=(kc == 0), stop=(kc == KC - 1)),
                             reads=[bwga, bhT], writes=[bpb[4]], inc=(kc == KC - 1))
                    evacB(gaT[0:16, 0:N], pbank[4][0:16, 0:N], [bpb[4]], [bgaT], eng="dve")
                    for j, li in enumerate(tiles):
                        js = slice(128 * j, 128 * j + 128)
                        S.op("pe", lambda e, js=js: e.matmul(pbank[4][:, :], lhsT=gaT[:, js], rhs=wal[:, :], start=True, stop=True),
                             reads=[bgaT, bwal], writes=[bpb[4]])
                        S.op("act", lambda e: e.activation(out=e1[:], in_=pbank[4][:, :], func=AF.Exp, scale=-1.0),
                             reads=[bpb[4]], writes=[be1])
                        S.op("act", lambda e: e.activation(out=spt[:], in_=e1[:], func=AF.Ln, bias=epsc[:, 1:2]),
                             reads=[be1, bepsc], writes=[bspt])
                        S.op("pe", lambda e: e.matmul(pbank[4][:, :], lhsT=tri[:, 0, :], rhs=spt[:], start=True, stop=True),
                             reads=[btri, bspt], writes=[bpb[4]])
                        S.op("act", lambda e: e.activation(out=E3[:], in_=pbank[4][:, :], func=AF.Exp),
                             reads=[bpb[4]], writes=[bE3])
                        for h in range(4):
                            S.op("pe", lambda e, h=h: e.matmul(pbank[5][:, 128 * h:128 * h + 128], lhsT=spt[:, 128 * h:128 * h + 128],
                                                               rhs=tri[:, 1, :], start=True, stop=True),
                                 reads=[bspt, btri], writes=[bpb[5]], inc=(h == 3))
                        S.op("act", lambda e: e.activation(out=E1T[:], in_=pbank[5][:, :], func=AF.Exp),
                             reads=[bpb[5]], writes=[bE1T])
                        S.op("act", lambda e: e.activation(out=E2T[:], in_=pbank[5][:, :], func=AF.Exp, scale=-1.0),
                             reads=[bpb[5]], writes=[bE2T])
                        S.op("dve", lambda e, js=js: e.scalar_tensor_tensor(
                            out=qtl[:], in0=qTf[:, :, js], scalar=128.0 ** -0.5, in1=E1T[:].rearrange("p (h n) -> p h n", h=4),
                            op0=ALU.mult, op1=ALU.mult), reads=[bqTf, bE1T], writes=[bqtl])
                        S.op("dve", lambda e, js=js: e.tensor_tensor(
                            out=ktl[:], in0=kTf[:, :, js], in1=E2T[:].rearrange("p (h n) -> p h n", h=4), op=ALU.mult),
                            reads=[bkTf, bE2T], writes=[bktl])
                        proj_tm(wgk, bwgk, j, 0, 2)
                        S.op("dve", lambda e: e.tensor_tensor(out=khat[:], in0=pbank[2][:, :], in1=E3[:], op=ALU.mult),
                             reads=[bpb[2], bE3], writes=[bkhat])
                        for cb in range(2):
                            proj_tm(wgv, bwgv, j, cb, 2 + cb)
                            evacB(gvb[:, 512 * cb:512 * cb + 512], pbank[2 + cb][:, :], [bpb[2 + cb]], [bgvb])
                        for cb in range(2):
                            proj_tm(wgg, bwgg, j, cb, 2 + cb)
                            S.op("act", lambda e, cb=cb: e.activation(out=gate[:, 512 * cb:512 * cb + 512], in_=pbank[2 + cb][:, :],
                                                                      func=AF.Silu),
                                 reads=[bpb[2 + cb]], writes=[bgate])
                        for h in range(4):
                            S.op("pe", lambda e, h=h: e.matmul(pbank[4][:, 128 * h:128 * h + 128], lhsT=ktl[:, h, :], rhs=qtl[:, h, :],
                                                               start=True, stop=True),
                                 reads=[bktl, bqtl], writes=[bpb[4]], inc=(h == 3))
                        S.op("dve", lambda e: e.tensor_tensor(out=AT[:], in0=pbank[4][:, :].rearrange("p (h n) -> p h n", h=4),
                                                              in1=mask4[:], op=ALU.mult),
                             reads=[bpb[4], bm4], writes=[bAT])
                        for h in range(4):
                            ob = 6 + h // 2
                            osl = slice(256 * (h % 2), 256 * (h % 2) + 256)
                            S.op("pe", lambda e, h=h, ob=ob, osl=osl: e.matmul(pbank[ob][:, osl], lhsT=AT[:, h, :],
                                                                               rhs=gvb[:, 256 * h:256 * h + 256], start=True, stop=False),
                                 reads=[bAT, bgvb], writes=[bpb[ob]], inc=False)
                            S.op("pe", lambda e, h=h, ob=ob, osl=osl: e.matmul(pbank[ob][:, osl], lhsT=qtl[:, h, :],
                                                                               rhs=Sbb[:, 256 * h:256 * h + 256], start=False, stop=True),
                                 reads=[bqtl, bSb], writes=[bpb[ob]])
                        for h in range(4):
                            usl = slice(256 * (h % 2), 256 * (h % 2) + 256)
                            S.op("pe", lambda e, h=h, usl=usl: e.matmul(pbank[5][:, usl], lhsT=khat[:, 128 * h:128 * h + 128],
                                                                        rhs=gvb[:, 256 * h:256 * h + 256], start=True, stop=True),
                                 reads=[bkhat, bgvb], writes=[bpb[5]])
                            S.op("dve", lambda e, h=h, usl=usl: e.scalar_tensor_tensor(
                                out=Sst[:, 256 * h:256 * h + 256], in0=Sst[:, 256 * h:256 * h + 256],
                                scalar=E1T[:, 128 * h + 127:128 * h + 128], in1=pbank[5][:, usl], op0=ALU.mult, op1=ALU.add),
                                reads=[bS, bE1T, bpb[5]], writes=[bS])
                        if li == 0:
                            S.op("dve", lambda e: e.tensor_scalar_mul(out=Sst[:], in0=Sst[:], scalar1=flagt[:, 0:1]),
                                 reads=[bS, bflag], writes=[bS])
                        S.op("act", lambda e: e.activation(out=Sbb[:], in_=Sst[:], func=AF.Copy), reads=[bS], writes=[bSb])
                        S.op("pool", lambda e: e.memset(oss[:], 0.0), writes=[boss])
                        om = omA[omi_c[0] % 2]; bom = bomA[omi_c[0] % 2]; omi_c[0] += 1
                        for h in range(4):
                            ob = 6 + h // 2
                            osl = slice(256 * (h % 2), 256 * (h % 2) + 256)
                            S.op("act", lambda e, h=h, ob=ob, osl=osl, om=om: e.activation(
                                out=om[:, 256 * h:256 * h + 256], in_=pbank[ob][:, osl], func=AF.Square, accum_out=oss[:, h:h + 1]),
                                reads=[bpb[ob]], writes=[bom, boss])
                            S.op("dve", lambda e, h=h: e.tensor_tensor(out=gate[:, 256 * h:256 * h + 256], in0=gate[:, 256 * h:256 * h + 256],
                                                                       in1=glab[:], op=ALU.mult), reads=[bgate, bglab], writes=[bgate])
                        S.op("act", lambda e: e.activation(out=oss[:, 4:8], in_=oss[:, 0:4], func=AF.Ln, scale=1.0 / 256, bias=epsc[:, 0:1]),
                             reads=[boss, bepsc], writes=[boss])
                        S.op("act", lambda e: e.activation(out=oss[:, 4:8], in_=oss[:, 4:8], func=AF.Exp, scale=-0.5),
                             reads=[boss], writes=[boss])
                        for h in range(4):
                            ob = 6 + h // 2
                            osl = slice(256 * (h % 2), 256 * (h % 2) + 256)
                            S.op("dve", lambda e, h=h, ob=ob, osl=osl, om=om: e.scalar_tensor_tensor(
                                out=om[:, 256 * h:256 * h + 256], in0=pbank[ob][:, osl], scalar=oss[:, 4 + h:5 + h],
                                in1=gate[:, 256 * h:256 * h + 256], op0=ALU.mult, op1=ALU.mult),
                                reads=[bpb[ob], boss, bgate], writes=[bom])
                        S.dma("sp", OM[128 * li:128 * li + 128, 0:1024], om[:], bom, reads=[bom], writes=[bOM])
                for tiles_ in own_macros:
                    do_macro_B(tiles_)
                if debug:
                    dq_ = dout("dbg_QT", [8, 128, S_OWN], BF16)
                    do_ = dout("dbg_OMA", [S_OWN, D], BF16)
                    bd = S.buf("dbgB")
                    S.barrier()
                    S.dma("sp", dq_, QT, bd, reads=[bQT])
                    S.dma("sp", do_, OM, bd, reads=[bOM])
                S.barrier()
                print("stage B sbuf remaining", nc.sbuf_bytes_remaining, "insts", S.ninst)
                S.emit()

        if "C" in stages:
            groups = [[0]] + [[2 * g - 1, 2 * g] for g in range(1, NG)]
            PK = min(16, NT)
            NP = NT // PK
            with ExitStack() as st:
                sb = lambda n, s, d: st.enter_context(nc.sbuf_tensor("C_" + n, list(s), d))
                KTs = sb("KTs", [128, 2, S_ALL], BF16)
                bK = [[S.buf("K%d_%d" % (c_, p_)) for p_ in range(NP)] for c_ in range(2)]
                Vs = sb("Vs", [128, NT, 257], BF16)
                bV = [S.buf("V%d" % p_) for p_ in range(NP)]
                posr = sb("posr", [128, NG * NT], F32); bposr = S.buf("posr")
                biasH = sb("biasH", [128, NG * NT], F32); bbias = S.buf("biasH")
                mkp = sb("mkp", [128, 16, 256], BF16); bmkp = S.buf("mkp")
                mkh = sb("mkh", [128, 8, 128], BF16); bmkh = S.buf("mkh")
                QTs = [sb("QTs%d" % i, [128, 2, 256], BF16) for i in range(2)]; bQTs = [S.buf("QTs%d" % i) for i in range(2)]
                PT = [sb("PT%d" % i, [128, 256], BF16) for i in range(4)]; bPT = [S.buf("PT%d" % i) for i in range(4)]
                lv = sb("lv", [1, 4, 128], F32); blv = S.buf("lv")
                lpr = sb("lpr", [1, 2, 128], F32); blpr = S.buf("lpr")
                ls = sb("ls", [1, 4], F32); bls = S.buf("ls")
                ones1 = sb("ones1", [1, 128], F32); bones = S.buf("ones1")
                lamc = sb("lamc", [128, 1], F32); blamc = S.buf("lamc")
                dnbc = sb("dnbc", [128, 256], F32); bdn = S.buf("dnbc")
                rr = sb("rr", [128, 4], F32); brr = S.buf("rr")
                t2 = sb("t2", [128, 256], F32); bt2 = S.buf("t2")
                of = sb("of", [128, 256], F32); bof = S.buf("of")
                osq = sb("osq", [128, 256], F32); bosq = S.buf("osq")
                ss2 = sb("ss2", [128, 2], F32); bss2 = S.buf("ss2")
                obf = [sb("obf%d" % i, [128, 256], BF16) for i in range(2)]; bobf = [S.buf("obf%d" % i) for i in range(2)]
                bpb = [S.buf("pbC%d" % i) for i in range(8)]
                bst = [bpb[4 + i] for i in range(4)]

                S.dma("sp", posr[:], posrel_d, bposr, writes=[bposr])
                S.dma("sp", mkp[:], maskp_d, bmkp, writes=[bmkp])
                S.dma("sp", mkh[:], maskh_d, bmkh, writes=[bmkh])
                S.dma("sp", lv[:], lamv.rearrange("(o a) n -> o a n", o=1), blv, writes=[blv])
                S.dma("sp", dnbc[:], hnrm[1:2, :].partition_broadcast(128), bdn, writes=[bdn])
                S.op("act", lambda e: e.mul(out=dnbc[:], in_=dnbc[:], mul=1.0 - LAMBDA_INIT), reads=[bdn], writes=[bdn])
                S.op("pool", lambda e: e.memset(ones1[:], 1.0), writes=[bones])
                S.op("pool", lambda e: e.memset(Vs[:, :, 256:257], 1.0), writes=bV)
                S.op("dve", lambda e: e.tensor_tensor(out=lpr[:, 0, :], in0=lv[:, 0, :], in1=lv[:, 1, :], op=ALU.mult),
                     reads=[blv], writes=[blpr])
                S.op("dve", lambda e: e.tensor_tensor(out=lpr[:, 1, :], in0=lv[:, 2, :], in1=lv[:, 3, :], op=ALU.mult),
                     reads=[blv], writes=[blpr])
                S.op("dve", lambda e: e.reduce_sum(out=ls[:, 0:2], in_=lpr[:], axis=mybir.AxisListType.X),
                     reads=[blpr], writes=[bls])
                S.op("act", lambda e: e.activation(out=ls[:, 2:4], in_=ls[:, 0:2], func=AF.Exp), reads=[bls], writes=[bls])
                S.op("dve", lambda e: e.tensor_tensor(out=ls[:, 0:1], in0=ls[:, 2:3], in1=ls[:, 3:4], op=ALU.subtract),
                     reads=[bls], writes=[bls])
                S.op("dve", lambda e: e.tensor_scalar_add(out=ls[:, 0:1], in0=ls[:, 0:1], scalar1=LAMBDA_INIT),
                     reads=[bls], writes=[bls])
                S.op("pe", lambda e: e.matmul(pbank[7][:, 0:1], lhsT=ones1[:], rhs=ls[:, 0:1], start=True, stop=True),
                     reads=[bones, bls], writes=[bpb[7]])
                S.op("dve", lambda e: e.tensor_copy(out=lamc[:], in_=pbank[7][:, 0:1]), reads=[bpb[7]], writes=[blamc])

                qsl = 0
                ptc = 0
                stc = 0
                obc = 0
                for h in range(4):
                    for comp in range(2):
                        for p_ in range(NP):
                            S.dma("sp", KTs[:, comp, 128 * PK * p_:128 * PK * (p_ + 1)],
                                  KT[2 * h + comp, :, 128 * PK * p_:128 * PK * (p_ + 1)], bK[comp][p_],
                                  reads=[bKT], writes=[bK[comp][p_]])
                    for p_ in range(NP):
                        S.dma("sp", Vs[:, PK * p_:PK * (p_ + 1), 0:256],
                              VV[128 * PK * p_:128 * PK * (p_ + 1), 256 * h:256 * h + 256].rearrange("(t p) e -> p t e", p=128),
                              bV[p_], reads=[bVV], writes=[bV[p_]])
                    S.op("dve", lambda e, h=h: e.tensor_scalar_mul(out=biasH[:], in0=posr[:], scalar1=float(SLOPES[h])),
                         reads=[bposr], writes=[bbias])
                    for g, tl in enumerate(groups):
                        nq = 128 * len(tl)
                        q0 = 128 * tl[0]
                        qs = qsl % 2; qsl += 1
                        for comp in range(2):
                            S.dma("sp", QTs[qs][:, comp, 0:nq], QT[2 * h + comp, :, q0:q0 + nq], bQTs[qs],
                                  reads=[bQT], writes=[bQTs[qs]])
                        if g == 0:
                            cand = {0: 0}
                            for cp in range(1, 8):
                                cand[nb * cp - 1] = cp
                        else:
                            cand = {}
                            for cp in range(8):
                                for ab in range(2):
                                    cand[nb * cp + 2 * g - 2 + ab] = 2 * cp + ab
                        KTN = min(NT, 7 * nb + tl[-1])
                        its = [(kt, comp) for kt in range(KTN) for comp in range(2)]
                        LOOK = int(os.environ.get("LOOK", "3"))
                        slots = {}

                        def emit_qk(i):
                            nonlocal stc
                            kt, comp = its[i]
                            p_ = kt // PK
                            sslot = stc % 4; stc += 1
                            sps = pbank[4 + sslot][:, 0:nq]
                            slots[i] = (sslot, sps)
                            S.op("pe", lambda e, sps=sps, comp=comp, kt=kt, qs=qs, nq=nq: e.matmul(
                                sps, lhsT=KTs[:, comp, 128 * kt:128 * kt + 128], rhs=QTs[qs][:, comp, 0:nq], start=True, stop=True),
                                reads=[bK[comp][p_], bQTs[qs]], writes=[bst[sslot]])

                        def emit_rest(i):
                            nonlocal ptc
                            kt, comp = its[i]
                            p_ = kt // PK
                            sslot, sps = slots.pop(i)
                            pi = ptc % 4; ptc += 1
                            bidx = g * NT + kt
                            S.op("act", lambda e, sps=sps, pi=pi, bidx=bidx, nq=nq: e.activation(
                                out=PT[pi][:, 0:nq], in_=sps, func=AF.Exp, bias=biasH[:, bidx:bidx + 1]),
                                reads=[bst[sslot], bbias], writes=[bPT[pi]])
                            if kt in cand:
                                mk = mkh[:, cand[kt], 0:nq] if g == 0 else mkp[:, cand[kt], 0:nq]
                                bmk = bmkh if g == 0 else bmkp
                                S.op("dve", lambda e, pi=pi, mk=mk, nq=nq: e.tensor_tensor(
                                    out=PT[pi][:, 0:nq], in0=PT[pi][:, 0:nq], in1=mk, op=ALU.mult),
                                    reads=[bPT[pi], bmk], writes=[bPT[pi]])
                            for ti in range(len(tl)):
                                ob = 2 * ti + comp
                                S.op("pe", lambda e, ob=ob, pi=pi, ti=ti, kt=kt, KTN=KTN: e.matmul(
                                    pbank[ob][:, 0:257], lhsT=PT[pi][:, 128 * ti:128 * ti + 128], rhs=Vs[:, kt, :],
                                    start=(kt == 0), stop=(kt == KTN - 1)),
                                    reads=[bPT[pi], bV[p_]], writes=[bpb[ob]], inc=(ti == len(tl) - 1))

                        for i in range(len(its) + LOOK):
                            if i < len(its):
                                emit_qk(i)
                            if i >= LOOK:
                                emit_rest(i - LOOK)
                        for ti, li in enumerate(tl):
                            O1 = pbank[2 * ti]; O2 = pbank[2 * ti + 1]
                            b1 = bpb[2 * ti]; b2 = bpb[2 * ti + 1]
                            S.op("dve", lambda e, O1=O1: e.reciprocal(out=rr[:, 0:1], in_=O1[:, 256:257]), reads=[b1], writes=[brr])
                            S.op("dve", lambda e, O2=O2: e.reciprocal(out=rr[:, 1:2], in_=O2[:, 256:257]), reads=[b2], writes=[brr])
                            S.op("dve", lambda e: e.tensor_tensor(out=rr[:, 2:3], in0=rr[:, 1:2], in1=lamc[:], op=ALU.mult),
                                 reads=[brr, blamc], writes=[brr])
                            S.op("dve", lambda e, O2=O2: e.tensor_scalar_mul(out=t2[:], in0=O2[:, 0:256], scalar1=rr[:, 2:3]),
                                 reads=[b2, brr], writes=[bt2])
                            S.op("dve", lambda e, O1=O1: e.scalar_tensor_tensor(out=of[:], in0=O1[:, 0:256], scalar=rr[:, 0:1], in1=t2[:],
                                                                               op0=ALU.mult, op1=ALU.subtract),
                                 reads=[b1, brr, bt2], writes=[bof])
                            S.op("pool", lambda e: e.memset(ss2[:], 0.0), writes=[bss2])
                            S.op("act", lambda e: e.activation(out=osq[:], in_=of[:], func=AF.Square, accum_out=ss2[:, 0:1]),
                                 reads=[bof], writes=[bosq, bss2])
                            rstd_ops(ss2, bss2, 256)
                            oi = obc % 2; obc += 1
                            S.op("dve", lambda e, oi=oi: e.scalar_tensor_tensor(out=obf[oi][:], in0=of[:], scalar=ss2[:, 1:2], in1=dnbc[:],
                                                                               op0=ALU.mult, op1=ALU.mult),
                                 reads=[bof, bss2, bdn], writes=[bobf[oi]])
                            S.dma("sp", OM[128 * li:128 * li + 128, 1024 + 256 * h:1024 + 256 * h + 256], obf[oi][:], bobf[oi],
                                  reads=[bobf[oi]], writes=[bOM])
                if debug:
                    do2 = dout("dbg_OM", [S_OWN, D], BF16)
                    bd = S.buf("dbgC")
                    S.barrier()
                    S.dma("sp", do2, OM, bd, reads=[bOM])
                S.barrier()
                print("stage C sbuf remaining", nc.sbuf_bytes_remaining, "insts", S.ninst)
                S.emit()

        if "D" in stages:
            with ExitStack() as stDE:
                h2T = stDE.enter_context(nc.sbuf_tensor("h2T", [128, KC, S_OWN], BF16)); bh2T = S.buf("h2T")
                identE = stDE.enter_context(nc.sbuf_tensor("identE", [128, 128], BF16)); bidE = S.buf("identE")
                S.dma("sp", identE[:], ident_d, bidE, writes=[bidE])
                with ExitStack() as st:
                    sb = lambda n, s, d: st.enter_context(nc.sbuf_tensor("D_" + n, list(s), d))
                    wo = sb("wo", [128, KC, D], BF16); bwo = S.buf("wo")
                    g1bc = sb("g1bc", [128, D], F32); bg1 = S.buf("g1bc")
                    g2bc = sb("g2bc", [128, D], F32); bg2 = S.buf("g2bc")
                    om = [sb("om%d" % i, [128, D], BF16) for i in range(2)]; bom = [S.buf("om%d" % i) for i in range(2)]
                    omT = sb("omT", [128, KC, 128], BF16); bomT = S.buf("omT")
                    xs = sb("xs", [128, D], F32); bxs = S.buf("xs")
                    x1 = sb("x1", [128, D], F32); bx1 = S.buf("x1")
                    tmp = sb("tmp", [128, D], F32); btmp = S.buf("tmp")
                    h2 = sb("h2", [128, D], BF16); bh2 = S.buf("h2")
                    s4 = sb("s4", [128, 8], F32); bs4 = S.buf("s4")
                    bpb = [S.buf("pbD%d" % i) for i in range(8)]
                    S.dma("sp", g1bc[:], nrm[1:2, :].partition_broadcast(128), bg1, writes=[bg1])
                    S.dma("sp", g2bc[:], nrm[2:3, :].partition_broadcast(128), bg2, writes=[bg2])
                    for kc in range(KC):
                        S.dma("pool", wo[:, kc, :], w_o.rearrange("(c p) n -> p c n", p=128)[:, kc, :], bwo, writes=[bwo])
                    ptr_bf = [pbank[0][:].bitcast(BF16), pbank[1][:].bitcast(BF16)]
                    tgd = [0]

                    def evacD(out, in_, reads, writes):
                        eng = ("act", "dve")[tgd[0] % 2]
                        tgd[0] += 1
                        if eng == "act":
                            return S.op("act", lambda e: e.activation(out=out, in_=in_, func=AF.Copy), reads=reads, writes=writes)
                        return S.op("dve", lambda e: e.tensor_copy(out=out, in_=in_), reads=reads, writes=writes)

                    for li in range(NOWN):
                        o_ = om[li % 2]; bo_ = bom[li % 2]
                        S.dma("sp", o_[:], OM[128 * li:128 * li + 128, :], bo_, reads=[bOM], writes=[bo_])
                        S.dma("sp", xs[:], x_own[128 * li:128 * li + 128, :], bxs, writes=[bxs])
                        for half in range(2):
                            pt = ptr_bf[half]
                            for k8 in range(8):
                                kc = half * 8 + k8
                                S.op("pe", lambda e, pt=pt, k8=k8, kc=kc, o_=o_: e.transpose(
                                    out=pt[:, 128 * k8:128 * k8 + 128], in_=o_[:, 128 * kc:128 * kc + 128], identity=identE[:]),
                                    reads=[bo_, bidE], writes=[bpb[half]], inc=(k8 == 7))
                            evacD(omT[:, half * 8:half * 8 + 8, :], pt.rearrange("p (k n) -> p k n", k=8), [bpb[half]], [bomT])
                        for cb in range(4):
                            for kc in range(KC):
                                S.op("pe", lambda e, cb=cb, kc=kc: e.matmul(pbank[4 + cb][:, :], lhsT=omT[:, kc, :],
                                                                            rhs=wo[:, kc, 512 * cb:512 * cb + 512],
                                                                            start=(kc == 0), stop=(kc == KC - 1)),
                                     reads=[bomT, bwo], writes=[bpb[4 + cb]], inc=(kc == KC - 1))
                        S.op("pool", lambda e: e.memset(s4[:], 0.0), writes=[bs4])
                        for cb in range(4):
                            S.op("act", lambda e, cb=cb: e.activation(out=tmp[:, 512 * cb:512 * cb + 512], in_=pbank[4 + cb][:, :],
                                                                      func=AF.Square, accum_out=s4[:, cb:cb + 1]),
                                 reads=[bpb[4 + cb]], writes=[btmp, bs4])
                        S.op("dve", lambda e: e.reduce_sum(out=s4[:, 4:5], in_=s4[:, 0:4], axis=mybir.AxisListType.X),
                             reads=[bs4], writes=[bs4])
                        S.op("act", lambda e: e.activation(out=s4[:, 5:6], in_=s4[:, 4:5], func=AF.Ln, scale=1.0 / D, bias=epsc[:, 0:1]),
                             reads=[bs4, bepsc], writes=[bs4])
                        S.op("act", lambda e: e.activation(out=s4[:, 5:6], in_=s4[:, 5:6], func=AF.Exp, scale=-0.5),
                             reads=[bs4], writes=[bs4])
                        for cb in range(4):
                            S.op("dve", lambda e, cb=cb: e.scalar_tensor_tensor(
                                out=tmp[:, 512 * cb:512 * cb + 512], in0=pbank[4 + cb][:, :], scalar=s4[:, 5:6],
                                in1=g1bc[:, 512 * cb:512 * cb + 512], op0=ALU.mult, op1=ALU.mult),
                                reads=[bpb[4 + cb], bs4, bg1], writes=[btmp])
                        S.op("dve", lambda e: e.tensor_tensor(out=x1[:], in0=tmp[:], in1=xs[:], op=ALU.add),
                             reads=[btmp, bxs], writes=[bx1])
                        S.dma("sp", X1[128 * li:128 * li + 128, :], x1[:], bx1, reads=[bx1], writes=[bX1])
                        S.op("pool", lambda e: e.memset(s4[:, 6:8], 0.0), writes=[bs4])
                        S.op("act", lambda e: e.activation(out=h2[:], in_=x1[:], func=AF.Square, accum_out=s4[:, 6:7]),
                             reads=[bx1], writes=[bh2, bs4])
                        S.op("act", lambda e: e.activation(out=s4[:, 7:8], in_=s4[:, 6:7], func=AF.Ln, scale=1.0 / D, bias=epsc[:, 0:1]),
                             reads=[bs4, bepsc], writes=[bs4])
                        S.op("act", lambda e: e.activation(out=s4[:, 7:8], in_=s4[:, 7:8], func=AF.Exp, scale=-0.5),
                             reads=[bs4], writes=[bs4])
                        S.op("dve", lambda e: e.scalar_tensor_tensor(out=h2[:], in0=x1[:], scalar=s4[:, 7:8], in1=g2bc[:],
                                                                    op0=ALU.mult, op1=ALU.mult),
                             reads=[bx1, bs4, bg2], writes=[bh2])
                        for half in range(2):
                            pt = ptr_bf[half]
                            for k8 in range(8):
                                kc = half * 8 + k8
                                S.op("pe", lambda e, pt=pt, k8=k8, kc=kc: e.transpose(
                                    out=pt[:, 128 * k8:128 * k8 + 128], in_=h2[:, 128 * kc:128 * kc + 128], identity=identE[:]),
                                    reads=[bh2, bidE], writes=[bpb[half]], inc=(k8 == 7))
                            evacD(h2T[:, half * 8:half * 8 + 8, 128 * li:128 * li + 128],
                                  pt.rearrange("p (k n) -> p k n", k=8), [bpb[half]], [bh2T])
                    if debug:
                        dx1 = dout("dbg_X1", [S_OWN, D], F32)
                        bd = S.buf("dbgD")
                        S.barrier()
                        S.dma("sp", dx1, X1, bd, reads=[bX1])
                    S.barrier()
                    print("stage D sbuf remaining", nc.sbuf_bytes_remaining, "insts", S.ninst)
                    S.emit()

                with ExitStack() as st:
                  if "E" in stages:
                      sb = lambda n, s, d: st.enter_context(nc.sbuf_tensor("E_" + n, list(s), d))
                      NW1 = 3
                      W1s = [sb("W1s%d" % i, [128, KC, 256], BF16) for i in range(NW1)]; bW1 = [S.buf("W1s%d" % i) for i in range(NW1)]
                      NW2 = 3
                      W2s = [sb("W2s%d" % i, [128, 1024], BF16) for i in range(NW2)]; bW2 = [S.buf("W2s%d" % i) for i in range(NW2)]
                      gT = sb("gT", [128, NFF, 512], BF16); bgT = S.buf("gT")
                      cvp = sb("cvp", [128, 4, NFF], F32); bcvp = S.buf("cvp")
                      flagt = sb("flagt", [128, 1], F32); bflag = S.buf("flagt")
                      g3bc = sb("g3bc", [128, D], F32); bg3 = S.buf("g3bc")
                      carry = sb("carry", [128, NFF, 2], F32); bcar = S.buf("carry")
                      cb_ = [sb("cbuf%d" % i, [128, 512], F32) for i in range(2)]; bcb = [S.buf("cbuf%d" % i) for i in range(2)]
                      gel = [sb("gel%d" % i, [128, 512], F32) for i in range(2)]; bgel = [S.buf("gel%d" % i) for i in range(2)]
                      fsave = sb("fsave", [128, 4, 2048], F32); bfs = S.buf("fsave")
                      x1t = [sb("x1t%d" % i, [128, D], F32) for i in range(2)]; bx1t = [S.buf("x1t%d" % i) for i in range(2)]
                      s8 = sb("s8", [128, 32], F32); bs8 = S.buf("s8")
                      bpb = [S.buf("pbE%d" % i) for i in range(8)]
                      S.dma("sp", cvp[:], convp, bcvp, writes=[bcvp])
                      S.dma("sp", flagt[:], flag_d, bflag, writes=[bflag])
                      S.dma("sp", g3bc[:], nrm[3:4, :].partition_broadcast(128), bg3, writes=[bg3])
                      ffn_macros = [list(range(i, min(i + 4, NOWN))) for i in range(1, NOWN, 4)]
                      w1c = [0]
                      w2c = [0]
                      cbc = [0]

                      def do_macro_E(mi, tiles):
                          N = 128 * len(tiles)
                          t0 = 128 * tiles[0]
                          for j in range(NFF):
                              ws = w1c[0] % NW1; w1c[0] += 1
                              S.dma("pool", W1s[ws][:, :, 0:128], w_f1[:, 128 * j:128 * j + 128].rearrange("(c p) n -> p c n", p=128),
                                    bW1[ws], writes=[bW1[ws]])
                              S.dma("pool", W1s[ws][:, :, 128:256],
                                    w_f1[:, DFF + 128 * j:DFF + 128 * j + 128].rearrange("(c p) n -> p c n", p=128),
                                    bW1[ws], writes=[bW1[ws]])
                              pa = 2 * (j % 2); pbb = pa + 1
                              if mi == 0:
                                  for kc in range(KC):
                                      S.op("pe", lambda e, ws=ws, kc=kc: e.matmul(pbank[4][:, 0:2], lhsT=W1s[ws][:, kc, 0:128],
                                                                                  rhs=h2T[:, kc, 126:128], start=(kc == 0), stop=(kc == KC - 1)),
                                           reads=[bW1[ws], bh2T], writes=[bpb[4]], inc=(kc == KC - 1))
                                  S.op("dve", lambda e, j=j: e.tensor_scalar_mul(out=carry[:, j, :], in0=pbank[4][:, 0:2], scalar1=flagt[:, 0:1]),
                                       reads=[bpb[4], bflag], writes=[bcar])
                              for kc in range(KC):
                                  S.op("pe", lambda e, ws=ws, kc=kc, pa=pa: e.matmul(pbank[pa][:, 0:N], lhsT=W1s[ws][:, kc, 0:128],
                                                                                     rhs=h2T[:, kc, t0:t0 + N], start=(kc == 0), stop=(kc == KC - 1)),
                                       reads=[bW1[ws], bh2T], writes=[bpb[pa]], inc=(kc == KC - 1))
                              for kc in range(KC):
                                  S.op("pe", lambda e, ws=ws, kc=kc, pbb=pbb: e.matmul(pbank[pbb][:, 0:N], lhsT=W1s[ws][:, kc, 128:256],
                                                                                       rhs=h2T[:, kc, t0:t0 + N], start=(kc == 0), stop=(kc == KC - 1)),
                                       reads=[bW1[ws], bh2T], writes=[bpb[pbb]], inc=(kc == KC - 1))
                              ci = cbc[0] % 2; cbc[0] += 1
                              cbuf = cb_[ci]; bc_ = bcb[ci]; ge = gel[ci]; bge = bgel[ci]
                              S.op("dve", lambda e, cbuf=cbuf, pa=pa, j=j: e.tensor_scalar(out=cbuf[:, 0:N], in0=pbank[pa][:, 0:N], scalar1=cvp[:, 2, j:j + 1],
                                                                                     scalar2=cvp[:, 3, j:j + 1], op0=ALU.mult, op1=ALU.add),
                                   reads=[bpb[pa], bcvp], writes=[bc_])
                              S.op("dve", lambda e, cbuf=cbuf, pa=pa, j=j: e.scalar_tensor_tensor(
                                  out=cbuf[:, 1:N], in0=pbank[pa][:, 0:N - 1], scalar=cvp[:, 1, j:j + 1], in1=cbuf[:, 1:N],
                                  op0=ALU.mult, op1=ALU.add), reads=[bpb[pa], bcvp, bc_], writes=[bc_])
                              S.op("dve", lambda e, cbuf=cbuf, pa=pa, j=j: e.scalar_tensor_tensor(
                                  out=cbuf[:, 2:N], in0=pbank[pa][:, 0:N - 2], scalar=cvp[:, 0, j:j + 1], in1=cbuf[:, 2:N],
                                  op0=ALU.mult, op1=ALU.add), reads=[bpb[pa], bcvp, bc_], writes=[bc_])
                              S.op("dve", lambda e, cbuf=cbuf, j=j: e.scalar_tensor_tensor(
                                  out=cbuf[:, 0:2], in0=carry[:, j, :], scalar=cvp[:, 0, j:j + 1], in1=cbuf[:, 0:2],
                                  op0=ALU.mult, op1=ALU.add), reads=[bcar, bcvp, bc_], writes=[bc_])
                              S.op("dve", lambda e, cbuf=cbuf, j=j: e.scalar_tensor_tensor(
                                  out=cbuf[:, 0:1], in0=carry[:, j, 1:2], scalar=cvp[:, 1, j:j + 1], in1=cbuf[:, 0:1],
                                  op0=ALU.mult, op1=ALU.add), reads=[bcar, bcvp, bc_], writes=[bc_])
                              S.op("dve", lambda e, pa=pa, j=j: e.tensor_copy(out=carry[:, j, :], in_=pbank[pa][:, N - 2:N]),
                                   reads=[bpb[pa], bc_], writes=[bcar])
                              S.op("act", lambda e, cbuf=cbuf, ge=ge: e.activation(out=ge[:, 0:N], in_=cbuf[:, 0:N], func=AF.Gelu_apprx_tanh),
                                   reads=[bc_], writes=[bge])
                              S.op("dve", lambda e, ge=ge, pbb=pbb, j=j: e.tensor_tensor(out=gT[:, j, 0:N], in0=pbank[pbb][:, 0:N], in1=ge[:, 0:N],
                                                                                        op=ALU.mult),
                                   reads=[bpb[pbb], bge], writes=[bgT])
                          if ELEVEL < 2:
                              return
                          S.op("pool", lambda e: e.memset(s8[:], 0.0), writes=[bs8])
                          for half in range(2):
                              for j in range(NFF):
                                  ws = w2c[0] % NW2; w2c[0] += 1
                                  S.dma("pool", W2s[ws][:], w_f2[128 * j:128 * j + 128, 1024 * half:1024 * half + 1024], bW2[ws], writes=[bW2[ws]])
                                  for ti in range(len(tiles)):
                                      for cb in range(2):
                                          S.op("pe", lambda e, ws=ws, ti=ti, cb=cb, j=j: e.matmul(
                                              pbank[2 * ti + cb][:, :], lhsT=gT[:, j, 128 * ti:128 * ti + 128], rhs=W2s[ws][:, 512 * cb:512 * cb + 512],
                                              start=(j == 0), stop=(j == NFF - 1)),
                                              reads=[bgT, bW2[ws]], writes=[bpb[2 * ti + cb]],
                                              inc=(ti == len(tiles) - 1 and cb == 1))
                              if ELEVEL < 2.5:
                                  continue
                              for ti, li in enumerate(tiles):
                                  for cb in range(2):
                                      fcol = 1024 * half + 512 * cb
                                      if cb == 0:
                                          S.op("dve", lambda e, ti=ti, cb=cb, fcol=fcol: e.tensor_copy(out=fsave[:, ti, fcol:fcol + 512],
                                                                                                     in_=pbank[2 * ti + cb][:, :]),
                                               reads=[bpb[2 * ti + cb]], writes=[bfs])
                                      else:
                                          S.op("act", lambda e, ti=ti, cb=cb, fcol=fcol: e.activation(out=fsave[:, ti, fcol:fcol + 512],
                                                                                                    in_=pbank[2 * ti + cb][:, :], func=AF.Copy),
                                               reads=[bpb[2 * ti + cb]], writes=[bfs])
                                      S.op("act", lambda e, ti=ti, cb=cb, half=half, fcol=fcol: e.activation(
                                          out=cb_[0][:], in_=fsave[:, ti, fcol:fcol + 512], func=AF.Square,
                                          accum_out=s8[:, 8 * ti + 2 * half + cb:8 * ti + 2 * half + cb + 1]),
                                          reads=[bfs], writes=[bcb[0], bs8])
                          if ELEVEL < 3:
                              return
                          for ti, li in enumerate(tiles):
                              xi = li % 2
                              S.dma("sp", x1t[xi][:], X1[128 * li:128 * li + 128, :], bx1t[xi], reads=[bX1], writes=[bx1t[xi]])
                              S.op("dve", lambda e, ti=ti: e.reduce_sum(out=s8[:, 8 * ti + 4:8 * ti + 5], in_=s8[:, 8 * ti:8 * ti + 4], axis=mybir.AxisListType.X),
                                   reads=[bs8], writes=[bs8])
                              S.op("act", lambda e, ti=ti: e.activation(out=s8[:, 8 * ti + 5:8 * ti + 6], in_=s8[:, 8 * ti + 4:8 * ti + 5], func=AF.Ln, scale=1.0 / D,
                                                                        bias=epsc[:, 0:1]), reads=[bs8, bepsc], writes=[bs8])
                              S.op("act", lambda e, ti=ti: e.activation(out=s8[:, 8 * ti + 5:8 * ti + 6], in_=s8[:, 8 * ti + 5:8 * ti + 6], func=AF.Exp, scale=-0.5),
                                   reads=[bs8], writes=[bs8])
                              for cb in range(4):
                                  S.op("dve", lambda e, ti=ti, cb=cb: e.scalar_tensor_tensor(
                                      out=fsave[:, ti, 512 * cb:512 * cb + 512], in0=fsave[:, ti, 512 * cb:512 * cb + 512],
                                      scalar=s8[:, 8 * ti + 5:8 * ti + 6], in1=g3bc[:, 512 * cb:512 * cb + 512],
                                      op0=ALU.mult, op1=ALU.mult),
                                      reads=[bfs, bs8, bg3], writes=[bfs])
                              S.op("dve", lambda e, ti=ti, xi=xi: e.tensor_tensor(out=x1t[xi][:], in0=x1t[xi][:], in1=fsave[:, ti, :], op=ALU.add),
                                   reads=[bfs, bx1t[xi]], writes=[bx1t[xi]])
                              S.dma("sp", y_out[128 * (li - 1):128 * li, :], x1t[xi][:], bx1t[xi], reads=[bx1t[xi]])
                      for mi_, tiles_ in enumerate(ffn_macros):
                          do_macro_E(mi_, tiles_)
                      if debug:
                          dF = dout("dbg_F", [128, 4, 2048], F32)
                          dG = dout("dbg_G", [128, NFF, 512], BF16)
                          bd = S.buf("dbgE")
                          S.barrier()
                          S.dma("sp", dF, fsave[:], bd, reads=[bfs])
                          S.dma("sp", dG, gT[:], bd, reads=[bgT])
                      S.barrier()
                      print("stage E sbuf remaining", nc.sbuf_bytes_remaining, "insts", S.ninst)
                      S.emit()

        S.barrier()
        S.emit()
    return nc, dbg


def _host_consts(nb, c):
    NT = NCORES * nb
    NG = 1 + nb // 2
    p = np.arange(128)
    def gt(li):
        T = nb * c + li - 1
        return 0 if T < 0 else T
    groups = [[0]] + [[2 * g - 1, 2 * g] for g in range(1, NG)]
    posrel = np.zeros((128, NG, NT), np.float32)
    for g, tl in enumerate(groups):
        Tl = [gt(li) for li in tl]
        qref = 128 * Tl[-1] + 127
        for kt in range(NT):
            if kt <= Tl[-1]:
                posrel[:, g, kt] = 128 * kt + p - qref
            else:
                posrel[:, g, kt] = NEG
    maskp = np.zeros((128, 16, 256), np.float32)
    q = np.arange(256)
    for cp in range(8):
        for ab in range(2):
            if cp < c:
                m = np.ones((128, 256), np.float32)
            elif cp > c:
                m = np.zeros((128, 256), np.float32)
            else:
                m = ((128 * ab + p)[:, None] <= q[None, :]).astype(np.float32)
            maskp[:, cp * 2 + ab, :] = m
    maskh = np.zeros((128, 8, 128), np.float32)
    qh = np.arange(128)
    Th = gt(0)
    for slot in range(8):
        kt = 0 if slot == 0 else nb * slot - 1
        kpos = 128 * kt + p
        qpos = 128 * Th + qh
        maskh[:, slot, :] = (kpos[:, None] <= qpos[None, :]).astype(np.float32)
    sel = np.zeros((128, 8), np.float32)
    sel[:, c] = 1.0
    flag = np.full((128, 1), 0.0 if c == 0 else 1.0, np.float32)
    return dict(posrel=posrel.reshape(128, NG * NT), maskp=maskp.astype(bf16_np), maskh=maskh.astype(bf16_np),
                sel=sel, flag=flag)


def make_in_maps(inputs, nb):
    x = np.asarray(inputs["x"], np.float32)[0]
    S_ALL = NCORES * nb * 128
    assert x.shape[0] == S_ALL
    w_in = np.ascontiguousarray(np.asarray(inputs["w_in"], np.float32)[0])
    walpha = np.concatenate([np.asarray(inputs["w_alpha_up"], np.float32)[0],
                             np.asarray(inputs["b_alpha"], np.float32)[0][None, :]], axis=0)
    nrm = np.stack([np.asarray(inputs[k], np.float32)[0] for k in
                    ("attn_pre_norm", "attn_post_norm", "ffn_pre_norm", "ffn_post_norm")])
    hnrm = np.stack([np.asarray(inputs["gla_norm"], np.float32)[0], np.asarray(inputs["diff_norm"], np.float32)[0]])
    lamv = np.stack([np.asarray(inputs[k], np.float32)[0] for k in ("lambda_q1", "lambda_k1", "lambda_q2", "lambda_k2")])
    cw = np.asarray(inputs["conv_w"], np.float32)[0]
    cb = np.asarray(inputs["conv_b"], np.float32)[0]
    convp = np.stack([cw[0], cw[1], cw[2], cb]).reshape(4, NFF, 128).transpose(2, 0, 1).copy()
    ident = np.eye(128, dtype=np.float32).astype(bf16_np)
    j = np.arange(128)[:, None]
    i = np.arange(128)[None, :]
    tri = np.stack([(j > i) * (-1.0 / 16), (j <= i) * (-1.0 / 16), (j <= i) * 1.0, np.zeros((128, 128))]).astype(np.float32)
    common = dict(x_all=x, w_in=w_in, walpha=walpha, w_o=np.asarray(inputs["w_o"], np.float32)[0],
                  w_f1=np.asarray(inputs["w_ffn_in"], np.float32)[0], w_f2=np.asarray(inputs["w_ffn_out"], np.float32)[0],
                  nrm=nrm, hnrm=hnrm, lamv=lamv, convp=convp, ident=ident, tri=tri)
    maps = []
    for c in range(NCORES):
        lo = (nb * c - 1) * 128
        if c == 0:
            xo = np.concatenate([x[0:128], x[0:nb * 128]], axis=0)
        else:
            xo = x[lo:lo + (nb + 1) * 128]
        m = dict(common)
        m["x_own"] = np.ascontiguousarray(xo)
        m.update(_host_consts(nb, c))
        maps.append(m)
    return maps


_CACHE = {}


def kernel(**inputs):
    nb = 16
    if nb not in _CACHE:
        _CACHE[nb] = build(nb)[0]
    nc = _CACHE[nb]
    maps = make_in_maps(inputs, nb)
    res = run_bass_kernel_spmd(nc, maps, core_ids=list(range(NCORES)))
    out = np.concatenate([np.asarray(res.results[c]["y"]) for c in range(NCORES)], axis=0)
    return out.reshape(1, NCORES * nb * 128, D).astype(np.float32)
```
